# Optimizing a Trainium2 kernel written in Bass

```python
import jax
import jax.numpy as jnp
from jax import lax
import numpy as np

D_MODEL = 4096
BATCH = 1
SEQ = 8192
DEPTH = 4

GRID_W = 64
CTX_LEN = 256
HEAD_DIM = 128
N_HEADS = D_MODEL // HEAD_DIM
N_KV_HEADS = N_HEADS // 4
WINDOW = 128
BLOCK = 128
ROPE_THETA = 10000.0
POOL_WINDOWS = (2, 4, 8, 16)
N_POOL_GROUPS = len(POOL_WINDOWS)
POOL_GROUP = D_MODEL // N_POOL_GROUPS
CONV_WIDTH = 3
D_FF = 4 * D_MODEL
ADALN_RANK = D_MODEL // 4
N_MIXERS = 3
N_ATTN = (DEPTH + 2) // 3
N_POOL = (DEPTH + 1) // 3
N_CONV = DEPTH // 3
DN_ALPHA = (2 * DEPTH) ** 0.25
DN_BETA = (8 * DEPTH) ** -0.25
LN_EPS = 1e-5

kernel_name = 'hybrid_interleaved_dit_ctx_prefix'


def layer_norm(x, g, b):
    xf = x.astype(jnp.float32)
    mu = jnp.mean(xf, axis=-1, keepdims=True)
    var = jnp.mean(jnp.square(xf - mu), axis=-1, keepdims=True)
    y = (xf - mu) * lax.rsqrt(var + LN_EPS) * g.astype(jnp.float32) + b.astype(jnp.float32)
    return y.astype(x.dtype)


def adaln(cond, w_down, w_up, bias):
    m = (jax.nn.silu(cond) @ w_down) @ w_up + bias
    return jnp.split(m, 6, axis=-1)


def axial_rope_tables(n_tokens):
    rows = n_tokens // GRID_W
    row_pos = jnp.repeat(jnp.arange(rows, dtype=jnp.float32), GRID_W)
    col_pos = jnp.tile(jnp.arange(GRID_W, dtype=jnp.float32), rows)
    n_freq = HEAD_DIM // 4
    freqs = ROPE_THETA ** (-jnp.arange(n_freq, dtype=jnp.float32) / n_freq)
    ang = jnp.stack([row_pos[:, None] * freqs, col_pos[:, None] * freqs], axis=1)
    return jnp.cos(ang), jnp.sin(ang)


def apply_axial_rope(x, cos, sin):
    b, s, h, d = x.shape
    xr = x.astype(jnp.float32).reshape(b, s, h, 2, 2, d // 4)
    x1, x2 = xr[..., 0, :], xr[..., 1, :]
    cb, sb = cos[None, :, None], sin[None, :, None]
    out = jnp.stack([x1 * cb - x2 * sb, x1 * sb + x2 * cb], axis=-2)
    return out.reshape(b, s, h, d).astype(x.dtype)


def windowed_gqa_attention(h, hc, w_qkv, w_o, sink, cos, sin, with_ctx_out):
    b, s, d = h.shape
    nb = s // BLOCK
    grp = N_HEADS // N_KV_HEADS
    scale = HEAD_DIM ** -0.5
    splits = [N_HEADS * HEAD_DIM, (N_HEADS + N_KV_HEADS) * HEAD_DIM]

    def project(u):
        n = u.shape[1]
        q, k, v = jnp.split(u @ w_qkv, splits, axis=-1)
        return (q.reshape(b, n, N_HEADS, HEAD_DIM),
                k.reshape(b, n, N_KV_HEADS, HEAD_DIM),
                v.reshape(b, n, N_KV_HEADS, HEAD_DIM))

    q, k, v = project(h)
    q = apply_axial_rope(q, cos, sin).reshape(b, s, N_KV_HEADS, grp, HEAD_DIM)
    k = apply_axial_rope(k, cos, sin)
    qc, kc, vc = project(hc)
    qc = qc.reshape(b, -1, N_KV_HEADS, grp, HEAD_DIM)
    sink_logit = sink.astype(jnp.float32).reshape(N_KV_HEADS, grp)[None, :, :, None, None]

    def softmax_with_sink(logits):
        sl = jnp.broadcast_to(sink_logit, logits.shape[:-1] + (1,))
        p = jax.nn.softmax(jnp.concatenate([logits, sl], axis=-1), axis=-1)
        return p[..., :-1]

    n_band = 3 * BLOCK
    kpad = jnp.pad(k, ((0, 0), (BLOCK, BLOCK), (0, 0), (0, 0)))
    vpad = jnp.pad(v, ((0, 0), (BLOCK, BLOCK), (0, 0), (0, 0)))
    q_blocks = jnp.moveaxis(q.reshape(b, nb, BLOCK, N_KV_HEADS, grp, HEAD_DIM), 1, 0)
    kj = jnp.arange(n_band)
    qi = jnp.arange(BLOCK)
    in_window = jnp.abs(kj[None, :] - BLOCK - qi[:, None]) <= WINDOW

    def one_block(args):
        n, qb = args
        kb = lax.dynamic_slice_in_dim(kpad, n * BLOCK, n_band, axis=1)
        vb = lax.dynamic_slice_in_dim(vpad, n * BLOCK, n_band, axis=1)
        key_pos = (n - 1) * BLOCK + kj
        valid = in_window & ((key_pos >= 0) & (key_pos < s))[None, :]
        s_band = jnp.einsum('bqkgd,bjkd->bkgqj', qb, kb).astype(jnp.float32) * scale
        s_band = jnp.where(valid, s_band, -jnp.inf)
        s_ctx = jnp.einsum('bqkgd,bckd->bkgqc', qb, kc).astype(jnp.float32) * scale
        p = softmax_with_sink(jnp.concatenate([s_band, s_ctx], axis=-1)).astype(vb.dtype)
        return (jnp.einsum('bkgqj,bjkd->bqkgd', p[..., :n_band], vb)
                + jnp.einsum('bkgqc,bckd->bqkgd', p[..., n_band:], vc))

    o = lax.map(one_block, (jnp.arange(nb), q_blocks))
    y = jnp.moveaxis(o, 0, 1).reshape(b, s, d) @ w_o
    if not with_ctx_out:
        return y, None
    s_cc = jnp.einsum('bqkgd,bckd->bkgqc', qc, kc).astype(jnp.float32) * scale
    p_cc = softmax_with_sink(s_cc).astype(vc.dtype)
    yc = jnp.einsum('bkgqc,bckd->bqkgd', p_cc, vc).reshape(b, -1, d) @ w_o
    return y, yc


def centred_mean_pool(u, window):
    s = u.shape[1]
    cs = jnp.pad(jnp.cumsum(u.astype(jnp.float32), axis=1), ((0, 0), (1, 0), (0, 0)))
    t = jnp.arange(s)
    lo = jnp.clip(t - window // 2, 0, s)
    hi = jnp.clip(t + window - window // 2, 0, s)
    total = jnp.take(cs, hi, axis=1) - jnp.take(cs, lo, axis=1)
    count = (hi - lo).astype(jnp.float32)
    return (total / count[None, :, None]).astype(u.dtype)


def multiscale_pool_mixer(u, w_pool, ch_scale):
    b, s, d = u.shape
    groups = jnp.split(u, N_POOL_GROUPS, axis=-1)
    diffs = jnp.stack([centred_mean_pool(gx, w) - gx for gx, w in zip(groups, POOL_WINDOWS)], axis=2)
    y = jnp.einsum('bsgc,gcd->bsgd', diffs, w_pool).reshape(b, s, d)
    return y * ch_scale


def short_conv(u, w, bias):
    d = u.shape[-1]
    pad = (CONV_WIDTH - 1) // 2
    y = lax.conv_general_dilated(u, w[:, None, :].astype(u.dtype), window_strides=(1,),
                                 padding=[(pad, CONV_WIDTH - 1 - pad)],
                                 dimension_numbers=('NWC', 'WIO', 'NWC'), feature_group_count=d)
    return y + bias


def gated_short_conv_mixer(u, w_in, conv_w, conv_b, w_out):
    b_gate, c_gate, x_in = jnp.split(u @ w_in, 3, axis=-1)
    return (b_gate * short_conv(c_gate * x_in, conv_w, conv_b)) @ w_out


def sqrelu_mlp(u, w1, w2):
    return jnp.square(jax.nn.relu(u @ w1)) @ w2


def setup_inputs(seed: int = 0) -> dict:
    key = jax.random.key(seed)
    ks = jax.random.split(key, 20)
    f32 = jnp.float32

    def nrm(k, shape, fan_in, gain=1.0):
        return jax.random.normal(k, shape, f32) * (gain * fan_in ** -0.5)

    qkv_width = (N_HEADS + 2 * N_KV_HEADS) * HEAD_DIM
    return {
        'x': jax.random.normal(ks[0], (BATCH, SEQ, D_MODEL), f32),
        'c': jax.random.normal(ks[1], (BATCH, D_MODEL), f32),
        'ctx': jax.random.normal(ks[2], (BATCH, CTX_LEN, D_MODEL), f32),
        'c_ctx': jax.random.normal(ks[3], (D_MODEL,), f32),
        'mod_w_down': nrm(ks[4], (DEPTH, D_MODEL, ADALN_RANK), D_MODEL),
        'mod_w_up': nrm(ks[5], (DEPTH, ADALN_RANK, 6 * D_MODEL), ADALN_RANK),
        'mod_b': 0.02 * jax.random.normal(ks[6], (DEPTH, 6 * D_MODEL), f32),
        'ln_g': 1.0 + 0.02 * jax.random.normal(ks[7], (DEPTH, 2, D_MODEL), f32),
        'ln_b': 0.02 * jax.random.normal(ks[8], (DEPTH, 2, D_MODEL), f32),
        'mlp_w1': nrm(ks[9], (DEPTH, D_MODEL, D_FF), D_MODEL),
        'mlp_w2': nrm(ks[10], (DEPTH, D_FF, D_MODEL), D_FF, DN_BETA),
        'attn_w_qkv': nrm(ks[11], (N_ATTN, D_MODEL, qkv_width), D_MODEL),
        'attn_w_o': nrm(ks[12], (N_ATTN, D_MODEL, D_MODEL), D_MODEL, DN_BETA),
        'attn_sink': 0.5 * jax.random.normal(ks[13], (N_ATTN, N_HEADS), f32),
        'pool_w': nrm(ks[14], (N_POOL, N_POOL_GROUPS, POOL_GROUP, POOL_GROUP), POOL_GROUP, DN_BETA),
        'pool_scale': 1.0 + 0.1 * jax.random.normal(ks[15], (N_POOL, D_MODEL), f32),
        'conv_w_in': nrm(ks[16], (N_CONV, D_MODEL, 3 * D_MODEL), D_MODEL),
        'conv_w': nrm(ks[17], (N_CONV, CONV_WIDTH, D_MODEL), CONV_WIDTH),
        'conv_b': 0.02 * jax.random.normal(ks[18], (N_CONV, D_MODEL), f32),
        'conv_w_out': nrm(ks[19], (N_CONV, D_MODEL, D_MODEL), D_MODEL, DN_BETA),
    }


def reference(x, c, ctx, c_ctx, mod_w_down, mod_w_up, mod_b, ln_g, ln_b, mlp_w1, mlp_w2,
              attn_w_qkv, attn_w_o, attn_sink, pool_w, pool_scale,
              conv_w_in, conv_w, conv_b, conv_w_out):
    cos, sin = axial_rope_tables(x.shape[1])
    for i in range(DEPTH):
        last = i == DEPTH - 1
        kind, j = i % N_MIXERS, i // N_MIXERS
        sh1, sc1, g1, sh2, sc2, g2 = [m[:, None, :] for m in adaln(c, mod_w_down[i], mod_w_up[i], mod_b[i])]
        sh1c, sc1c, g1c, sh2c, sc2c, g2c = adaln(c_ctx, mod_w_down[i], mod_w_up[i], mod_b[i])
        h = x * (1 + sc1) + sh1
        hc = ctx * (1 + sc1c) + sh1c
        if kind == 0:
            y, yc = windowed_gqa_attention(h, hc, attn_w_qkv[j], attn_w_o[j], attn_sink[j], cos, sin, not last)
        elif kind == 1:
            y = multiscale_pool_mixer(h, pool_w[j], pool_scale[j])
            yc = None if last else multiscale_pool_mixer(hc, pool_w[j], pool_scale[j])
        else:
            y = gated_short_conv_mixer(h, conv_w_in[j], conv_w[j], conv_b[j], conv_w_out[j])
            yc = None if last else gated_short_conv_mixer(hc, conv_w_in[j], conv_w[j], conv_b[j], conv_w_out[j])
        x = layer_norm(DN_ALPHA * x + g1 * y, ln_g[i, 0], ln_b[i, 0])
        h = x * (1 + sc2) + sh2
        x = layer_norm(DN_ALPHA * x + g2 * sqrelu_mlp(h, mlp_w1[i], mlp_w2[i]), ln_g[i, 1], ln_b[i, 1])
        if not last:
            ctx = layer_norm(DN_ALPHA * ctx + g1c * yc, ln_g[i, 0], ln_b[i, 0])
            hc = ctx * (1 + sc2c) + sh2c
            ctx = layer_norm(DN_ALPHA * ctx + g2c * sqrelu_mlp(hc, mlp_w1[i], mlp_w2[i]), ln_g[i, 1], ln_b[i, 1])
    return x
```

```python
import numpy as np
from contextlib import ExitStack
import concourse.bass as bass
import concourse.mybir as mybir
from concourse.bass_utils import run_bass_kernel_spmd

F32 = mybir.dt.float32
BF16 = mybir.dt.bfloat16
AF = mybir.ActivationFunctionType
ALU = mybir.AluOpType

D = 4096
KC = 32
SEQ = 8192
NCORES = 8
TOWN = SEQ // NCORES
NCTX = 256
DFF = 4 * D
DEPTH = 4
HD = 128
NH = 32
NKV = 8
RANK = 1024
DN_ALPHA = (2 * DEPTH) ** 0.25
LN_EPS = 1e-5
POOL_WINDOWS = (2, 4, 8, 16)
KINDS = [0, 1, 2, 0]
HALO = {0: 128, 1: 8, 2: 1}
CPAD = {0: 0, 1: 8, 2: 1}
NCO = 32


class Sem:
    def __init__(self, h, is_dma):
        self.h = h
        self.n = 0
        self.is_dma = is_dma


class Buf:
    __slots__ = ("name", "w", "r")

    def __init__(self, name):
        self.name = name
        self.w = None
        self.r = {}


class Env:
    def __init__(self, nc, stack):
        self.nc = nc
        self.stack = stack
        self.eng = {"pe": nc.tensor, "act": nc.scalar, "dve": nc.vector, "pool": nc.gpsimd, "sp": nc.sync}
        self.sems = []
        self.semd = {}
        self.esem = {k: self.new_sem("e_" + k, False) for k in ("pe", "act", "dve", "pool")}
        self.waited = {k: {} for k in self.eng}
        self.nbank = 0
        self.psb = [Buf("ps%d" % i) for i in range(8)]

    def new_sem(self, name, is_dma=True):
        if name in self.semd:
            return self.semd[name]
        h = self.stack.enter_context(self.nc.semaphore(name))
        s = Sem(h, is_dma)
        s.name = name
        self.sems.append(s)
        self.semd[name] = s
        return s

    def coll(self, fn, reads=(), writes=()):
        self._deps("pool", reads, writes)
        sem = self.new_sem("s_coll", False)
        ins = fn(self.eng["pool"])
        ins.then_inc(sem.h, 1)
        sem.n += 1
        ev = (sem, sem.n)
        self.wait("pool", ev)
        self._upd(ev, reads, writes)
        return ev

    def wait(self, en, ev):
        if ev is None:
            return
        sem, val = ev
        if sem.is_dma:
            val = sem.n
        w = self.waited[en]
        if w.get(sem, 0) >= val:
            return
        self.eng[en].wait_ge(sem.h, val)
        w[sem] = val

    def _deps(self, en, reads, writes):
        for b in reads:
            self.wait(en, b.w)
        for b in writes:
            self.wait(en, b.w)
            for s, v in b.r.items():
                self.wait(en, (s, v))

    def _upd(self, ev, reads, writes):
        s, v = ev
        for b in reads:
            if b.r.get(s, 0) < v:
                b.r[s] = v
        for b in writes:
            b.w = ev
            b.r = {}

    def op(self, en, fn, reads=(), writes=()):
        self._deps(en, reads, writes)
        ins = fn(self.eng[en])
        sem = self.esem[en]
        ins.then_inc(sem.h, 1)
        sem.n += 1
        ev = (sem, sem.n)
        self._upd(ev, reads, writes)
        return ev

    def dma(self, q, sem, out, in_, reads=(), writes=(), accum=False):
        self._deps(q, reads, writes)
        if accum:
            ins = self.eng[q].dma_start(out=out, in_=in_, accum_op=ALU.add)
        else:
            ins = self.eng[q].dma_start(out=out, in_=in_)
        ins.then_inc(sem.h, 16)
        sem.n += 16
        ev = (sem, sem.n)
        self._upd(ev, reads, writes)
        return ev

    def bank(self):
        b = self.nbank % 8
        self.nbank += 1
        return b

    def barrier(self):
        for en in self.eng:
            for s in self.sems:
                if s.name == "s_coll" and en != "pool":
                    continue
                if s.n > 0:
                    self.wait(en, (s, s.n))


class Ring:
    def __init__(self, E, name, tens):
        self.t = tens
        self.n = len(tens)
        self.b = [Buf("%s%d" % (name, i)) for i in range(self.n)]
        self.s = [E.new_sem("%s%d" % (name, i)) for i in range(self.n)]
        self.i = 0

    def next(self):
        i = self.i % self.n
        self.i += 1
        return self.t[i], self.b[i], self.s[i]


def chunks_of(total, step):
    out = []
    c = 0
    while c < total:
        n = min(step, total - c)
        out.append((c, n))
        c += n
    return out


class LayerProg:
    def __init__(self, nc, kind, last, li=0, E=None, fused=None):
        self.nc = nc
        self.kind = kind
        self.last = last
        self.li = li
        self.sfx = "_L%d" % li
        self.Eshared = E
        self.fused = fused
        self.nc_sbuf = lambda n, sh, dt: nc.sbuf_tensor(n + self.sfx, sh, dt)
        self.HL = HALO[kind]
        self.CP = CPAD[kind]
        self.XE = TOWN + 2 * self.HL
        self.CE = NCTX if kind == 0 else NCO + 2 * self.CP
        self.NC = 0 if last else NCO
        self.T = TOWN + self.NC
        self.tiles = [(0, 512, 0), (512, 512, 0)] + ([(1024, NCO, 1)] if self.NC else [])

    def declare(self):
        nc, kind = self.nc, self.kind
        sfx, fz, li = self.sfx, self.fused, self.li
        ein = lambda n, s, dt=F32: nc.dram_tensor(n + sfx, s, dt, kind="ExternalInput").ap()
        if fz is not None and li > 0:
            self.xT = fz["xext"][li][:, 128 - self.HL:128 + TOWN + self.HL]
            self.cT = fz["cext"][li][:, 8 - self.CP:8 + NCTX + self.CP]
        else:
            self.xT = ein("xT", [D, self.XE])
            self.cT = ein("cT", [D, self.CE])
        self.cc = ein("cc", [128, KC, 2])
        self.edge = ein("edge", [128, 2])
        self.w_down = ein("w_down", [D, RANK])
        self.w_up = ein("w_up", [RANK, 6 * D])
        self.modb = ein("modb", [128, 6 * KC])
        self.lng = ein("lng", [128, 2, KC])
        self.lnb = ein("lnb", [128, 2, KC])
        self.w1 = ein("w1", [D, DFF])
        self.w2 = ein("w2", [DFF, D])
        if kind == 0:
            self.wqkv = ein("wqkv", [D, 6144])
            self.wo = ein("wo", [D, D])
            self.sinkb = ein("sinkb", [128, NH])
            self.cos = ein("cos", [128, 1280])
            self.sin = ein("sin", [128, 1280])
            self.perm = ein("perm", [128, 128])
            self.masks = ein("masks", [128, 4, 512])
        elif kind == 1:
            self.poolw = ein("poolw", [4, 1024, 1024])
            self.pscale = ein("pscale", [128, KC])
            self.invc = ein("invc", [128, 4, self.XE + self.CE])
        else:
            self.win = ein("win", [D, 3 * D])
            self.convw = ein("convw", [128, 3, KC])
            self.convb = ein("convb", [128, KC])
            self.wout = ein("wout", [D, D])
        if fz is not None and not self.last:
            self.xo = fz["xext"][li + 1][:, 128:128 + TOWN]
            self.co = fz["cext"][li + 1][:, 8:8 + NCTX]
        else:
            self.xo = nc.dram_tensor("xo" + sfx, [D, TOWN], F32, kind="ExternalOutput").ap()
            if self.NC:
                self.co = nc.dram_tensor("co" + sfx, [D, NCO], F32, kind="ExternalOutput").ap()
        self.z = nc.dram_tensor("z" + sfx, [D, self.T], F32, kind="Internal").ap()
        self.hid = nc.dram_tensor("hid" + sfx, [DFF, self.T], BF16, kind="Internal").ap()
        if kind != 1:
            self.obuf = nc.dram_tensor("obuf" + sfx, [D, self.T], BF16, kind="Internal").ap()

    def build(self):
        nc = self.nc
        self.declare()
        with ExitStack() as st:
            E = self.E = self.Eshared if self.Eshared is not None else Env(nc, st)
            sb = lambda n, s, dt: st.enter_context(self.nc_sbuf(n, s, dt))
            self.ICOLS = {0: 1536, 1: 1280, 2: 1284}[self.kind]
            self.insb = sb("insb", [128, KC, self.ICOLS], BF16)
            self.insb_b = [Buf("insb%d" % k) for k in range(KC)]
            self.s_insb = E.new_sem("s_insb")
            self.ps = st.enter_context(nc.psum_tensor("ps" + self.sfx, [128, 8, 512], F32))
            self.stg = Ring(E, "stg", [sb("stg%d" % i, [128, 512], F32) for i in range(4)])
            self.stgb = Ring(E, "stgb", [sb("stgb%d" % i, [128, 512], BF16) for i in range(4)])
            self.mod = sb("mod_sb", [128, 6 * KC, 2], F32)
            self.mod_b = Buf("mod")
            self.modb_sb = sb("modb_sb", [128, 6 * KC], F32)
            self.lng_sb = sb("lng_sb", [128, 2, KC], F32)
            self.lnb_sb = sb("lnb_sb", [128, 2, KC], F32)
            self.edge_sb = sb("edge_sb", [128, 2], F32)
            self.ones_f = sb("ones_f", [128, 128], F32)
            self.ones_b = sb("ones_b", [128, 128], BF16)
            self.cst_b = Buf("cst")
            self.s_cst = E.new_sem("s_cst")
            self.zb = [[Buf("z%d_%d" % (k, t)) for t in range(3)] for k in range(KC)]
            self.hidb = [Buf("hid%d" % m) for m in range(DFF // 128)]
            self.ob = [Buf("ob%d" % k) for k in range(KC)]
            def alloc_wsl(s2, sn=512):
                self._wn = getattr(self, "_wn", 0) + 1
                self.SN = sn
                self._pref = None
                wt = [s2.enter_context(self.nc_sbuf("wsl%d_%d" % (self._wn, i), [128, KC, sn], BF16)) for i in range(2)]
                self.wsl = Ring(E, "wsl", wt)

            with ExitStack() as s2:
                alloc_wsl(s2)
                self.load_consts()
                self.adaln()
                E.barrier()
            if self.kind == 1:
                self.prep_pool()
            else:
                self.prep()
            E.barrier()
            with ExitStack() as s2:
                alloc_wsl(s2, 256 if self.kind == 0 else 512)
                if self.kind == 0:
                    self.attention()
                elif self.kind == 1:
                    self.pool_mixer()
                else:
                    self.conv_mixer()
                E.barrier()
            self.layernorm(0)
            E.barrier()
            with ExitStack() as s2:
                alloc_wsl(s2)
                self.mlp()
                E.barrier()
            self.layernorm(1)
            E.barrier()
        return nc

    def mv(self, j, kc, sel):
        return self.mod[:, j * KC + kc, sel:sel + 1]

    def load_consts(self):
        E, nc = self.E, self.nc
        for dst, src in ((self.modb_sb, self.modb), (self.lng_sb, self.lng), (self.lnb_sb, self.lnb),
                         (self.edge_sb, self.edge)):
            E.dma("sp", self.s_cst, dst[:], src, writes=[self.cst_b])
        self.ones_buf = Buf("ones")
        E.op("dve", lambda e: e.memset(self.ones_f[:], 1.0), writes=[self.ones_buf])
        E.op("dve", lambda e: e.memset(self.ones_b[:], 1.0), writes=[self.ones_buf])

    def _load_slab(self, W, kcn, pieces):
        E = self.E
        Wv = W.rearrange("(kc p) n -> p kc n", p=128)
        t, b, s = self.wsl.next()
        off = 0
        for (c0, n) in pieces:
            E.dma("pool", s, t[:, 0:kcn, off:off + n], Wv[:, :, c0:c0 + n], writes=[b])
            off += n
        return t, b

    def gemm(self, W, kc0, kcn, slabs, tiles, epi, in_bufs=None, insb=None, mode=None, key=None, then=None):
        E = self.E
        insb = self.insb if insb is None else insb
        in_bufs = self.insb_b[kc0:kc0 + kcn] if in_bufs is None else in_bufs
        pf = self._pref
        if pf is not None and key is not None and pf[0] == key:
            nxt = pf[1]
        else:
            nxt = self._load_slab(W, kcn, slabs[0])
        self._pref = None
        for si in range(len(slabs)):
            wt, wb = nxt
            if si + 1 < len(slabs):
                nxt = self._load_slab(W, kcn, slabs[si + 1])
            elif then is not None:
                k2, W2, kcn2, pieces2 = then
                self._pref = (k2, self._load_slab(W2, kcn2, pieces2))
            ncol = sum(n for _, n in slabs[si])
            for ci in range(ncol // 128):
                md = mode(si, ci) if mode else "N"
                if md == "N":
                    for ti, (t0, tn, sel) in enumerate(tiles):
                        bk = E.bank()
                        pap = self.ps[:, bk, 0:tn]

                        def mm(e, pap=pap, ci=ci, t0=t0, tn=tn):
                            for k in range(kcn):
                                ins = e.matmul(pap, lhsT=wt[:, k, ci * 128:(ci + 1) * 128],
                                               rhs=insb[:, kc0 + k, t0:t0 + tn],
                                               start=(k == 0), stop=(k == kcn - 1))
                            return ins
                        E.op("pe", mm, reads=[wb] + in_bufs, writes=[E.psb[bk]])
                        epi(si, ci, ti, (t0, tn, sel), pap, E.psb[bk])
                else:
                    for ti, (t0, tn, sel) in enumerate(md):
                        bk = E.bank()
                        pap = self.ps[:, bk, 0:128]

                        def mm(e, pap=pap, ci=ci, t0=t0):
                            for k in range(kcn):
                                ins = e.matmul(pap, lhsT=insb[:, kc0 + k, t0:t0 + 128],
                                               rhs=wt[:, k, ci * 128:(ci + 1) * 128],
                                               start=(k == 0), stop=(k == kcn - 1))
                            return ins
                        E.op("pe", mm, reads=[wb] + in_bufs, writes=[E.psb[bk]])
                        epi(si, ci, ti, (t0, 128, sel), pap, E.psb[bk])

    def adaln(self):
        E, nc = self.E, self.nc
        with ExitStack() as st:
            ccs = st.enter_context(self.nc_sbuf("ccs", [128, KC, 2], F32))
            t1 = st.enter_context(self.nc_sbuf("t1", [128, 8, 2], BF16))
            ccb, t1b = Buf("ccs"), [Buf("t1_%d" % i) for i in range(8)]
            E.dma("sp", self.s_cst, ccs[:], self.cc, writes=[ccb])
            E.op("act", lambda e: e.activation(out=self.insb[:, :, 0:2], in_=ccs[:], func=AF.Silu),
                 reads=[ccb], writes=self.insb_b)

            def epi_down(si, ci, ti, tile, pap, pbuf):
                n = si * 4 + ci
                E.op("dve", lambda e: e.tensor_copy(out=t1[:, n, :], in_=pap), reads=[pbuf], writes=[t1b[n]])
            self.gemm(self.w_down, 0, KC, [[(c, 512)] for c in range(0, RANK, 512)], [(0, 2, 0)], epi_down,
                      then=("up", self.w_up, 8, [(0, 512)]))

            def epi_up(si, ci, ti, tile, pap, pbuf):
                n = si * 4 + ci
                E.op("dve", lambda e: e.tensor_scalar(out=self.mod[:, n, :], in0=pap, scalar1=self.modb_sb[:, n:n + 1],
                                                      scalar2=None, op0=ALU.add),
                     reads=[pbuf, self.cst_b], writes=[self.mod_b])
            self.gemm(self.w_up, 0, 8, [[(c, 512)] for c in range(0, 6 * D, 512)], [(0, 2, 0)], epi_up,
                      in_bufs=t1b, insb=t1, key="up")
            for j in (1, 4):
                E.op("dve", lambda e, j=j: e.tensor_scalar(out=self.mod[:, j * KC:(j + 1) * KC, :],
                                                           in0=self.mod[:, j * KC:(j + 1) * KC, :], scalar1=1.0,
                                                           scalar2=None, op0=ALU.add),
                     reads=[self.mod_b], writes=[self.mod_b])
            E.barrier()

    def prep(self):
        E, nc = self.E, self.nc
        XE, CE, HL, CP = self.XE, self.CE, self.HL, self.CP
        xv = self.xT.rearrange("(kc p) t -> p kc t", p=128)
        cv = self.cT.rearrange("(kc p) t -> p kc t", p=128)
        zv = self.z.rearrange("(kc p) t -> p kc t", p=128)
        with ExitStack() as st:
            xl = Ring(E, "xl", [st.enter_context(self.nc_sbuf("xl%d" % i, [128, XE + CE], F32)) for i in range(2)])
            zs = Ring(E, "zs", [st.enter_context(self.nc_sbuf("zs%d" % i, [128, TOWN + NCTX], F32)) for i in range(2)])
            def ldx(kc_):
                t_, b_, s_ = xl.next()
                E.dma("sp", s_, t_[:, 0:XE], xv[:, kc_, :], writes=[b_])
                E.dma("sp", s_, t_[:, XE:XE + CE], cv[:, kc_, :], writes=[b_])
                return t_, b_, s_
            nx = ldx(0)
            for kc in range(KC):
                t, b, s = nx
                if kc + 1 < KC:
                    nx = ldx(kc + 1)
                ib = self.insb_b[kc]
                E.op("act", lambda e: e.activation(out=self.insb[:, kc, 0:XE], in_=t[:, 0:XE], func=AF.Identity,
                                                   scale=self.mv(1, kc, 0), bias=self.mv(0, kc, 0)),
                     reads=[b, self.mod_b], writes=[ib])
                E.op("act", lambda e: e.activation(out=self.insb[:, kc, XE:XE + CE], in_=t[:, XE:XE + CE],
                                                   func=AF.Identity, scale=self.mv(1, kc, 1), bias=self.mv(0, kc, 1)),
                     reads=[b, self.mod_b], writes=[ib])
                E.op("dve", lambda e: e.tensor_scalar(out=self.insb[:, kc, 0:HL], in0=self.insb[:, kc, 0:HL],
                                                      scalar1=self.edge_sb[:, 0:1], scalar2=None, op0=ALU.mult),
                     reads=[self.cst_b], writes=[ib])
                E.op("dve", lambda e: e.tensor_scalar(out=self.insb[:, kc, HL + TOWN:XE], in0=self.insb[:, kc, HL + TOWN:XE],
                                                      scalar1=self.edge_sb[:, 1:2], scalar2=None, op0=ALU.mult),
                     reads=[self.cst_b], writes=[ib])
                if CP:
                    E.op("dve", lambda e: e.tensor_scalar(out=self.insb[:, kc, XE:XE + CP], in0=self.insb[:, kc, XE:XE + CP],
                                                          scalar1=self.edge_sb[:, 0:1], scalar2=None, op0=ALU.mult),
                         reads=[self.cst_b], writes=[ib])
                    E.op("dve", lambda e: e.tensor_scalar(out=self.insb[:, kc, XE + CP + NCO:XE + CE],
                                                          in0=self.insb[:, kc, XE + CP + NCO:XE + CE],
                                                          scalar1=self.edge_sb[:, 1:2], scalar2=None, op0=ALU.mult),
                         reads=[self.cst_b], writes=[ib])
                zt, zb_, zsem = zs.next()
                E.op("dve", lambda e: e.tensor_scalar(out=zt[:, 0:TOWN], in0=t[:, HL:HL + TOWN], scalar1=float(DN_ALPHA),
                                                      scalar2=None, op0=ALU.mult), reads=[b], writes=[zb_])
                if self.NC:
                    E.op("dve", lambda e: e.tensor_scalar(out=zt[:, TOWN:TOWN + NCO], in0=t[:, XE + CP:XE + CP + NCO],
                                                          scalar1=float(DN_ALPHA), scalar2=None, op0=ALU.mult),
                         reads=[b], writes=[zb_])
                E.dma("sp", zsem, zv[:, kc, :], zt[:, 0:self.T], reads=[zb_], writes=self.zb[kc])
            E.barrier()

    def acc_z(self, kc, ti, tile, pap, pbuf, gate_ap, extra_reads=()):
        E = self.E
        t0, tn, sel = tile
        st, sbf, ssem = self.stg.next()
        en = "act" if (self._acc_i % 2 == 0) else "dve"
        self._acc_i += 1
        if en == "act":
            E.op("act", lambda e: e.activation(out=st[:, 0:tn], in_=pap, func=AF.Identity, scale=gate_ap),
                 reads=[pbuf, self.mod_b] + list(extra_reads), writes=[sbf])
        else:
            E.op("dve", lambda e: e.tensor_scalar(out=st[:, 0:tn], in0=pap, scalar1=gate_ap, scalar2=None, op0=ALU.mult),
                 reads=[pbuf, self.mod_b] + list(extra_reads), writes=[sbf])
        zv = self.z[kc * 128:(kc + 1) * 128, t0:t0 + tn]
        E.dma("pool", ssem, zv, st[:, 0:tn], reads=[sbf], writes=[self.zb[kc][ti]], accum=True)

    _acc_i = 0
    _pref = None

    def mlp(self):
        E, nc = self.E, self.nc
        T = self.T
        hidv = self.hid.rearrange("(m p) t -> p m t", p=128)

        def epi1(si, ci, ti, tile, pap, pbuf):
            t0, tn, sel = tile
            m = si * 4 + ci
            st, sbf, ssem = self.stg.next()
            E.op("act", lambda e: e.activation(out=st[:, 0:tn], in_=pap, func=AF.Relu), reads=[pbuf], writes=[sbf])
            ob, obf, osem = self.stgb.next()
            E.op("dve", lambda e: e.tensor_tensor(out=ob[:, 0:tn], in0=st[:, 0:tn], in1=st[:, 0:tn], op=ALU.mult),
                 reads=[sbf], writes=[obf])
            E.dma("sp", osem, self.hid[m * 128:(m + 1) * 128, t0:t0 + tn], ob[:, 0:tn], reads=[obf], writes=[self.hidb[m]])
        self.gemm(self.w1, 0, KC, [[(c, 512)] for c in range(0, DFF, 512)], self.tiles, epi1,
                  then=("w2_0", self.w2[0:16 * 128, :], 16, [(0, 512)]))
        E.barrier()
        KB = 16
        nblk = DFF // (128 * KB)

        def load_blk(b):
            half = (b % 2) * KB
            E.dma("sp", self.s_insb, self.insb[:, half:half + KB, 0:T], hidv[:, b * KB:(b + 1) * KB, :],
                  reads=self.hidb[b * KB:(b + 1) * KB], writes=self.insb_b[half:half + KB])
        load_blk(0)
        for b in range(nblk):
            if b + 1 < nblk:
                load_blk(b + 1)
            half = (b % 2) * KB

            def epi2(si, ci, ti, tile, pap, pbuf):
                n = si * 4 + ci
                self.acc_z(n, ti, tile, pap, pbuf, self.mv(5, n, tile[2]))
            then = ("w2_%d" % (b + 1), self.w2[(b + 1) * KB * 128:(b + 2) * KB * 128, :], KB, [(0, 512)]) if b + 1 < nblk else None
            self.gemm(self.w2[b * KB * 128:(b + 1) * KB * 128, :], half, KB, [[(c, 512)] for c in range(0, D, 512)],
                      self.tiles, epi2, key="w2_%d" % b, then=then)

    def layernorm(self, which):
        E, nc = self.E, self.nc
        zv = self.z.rearrange("(kc p) t -> p kc t", p=128)
        ln_tiles = [(c, 256, 0) for c in range(0, TOWN, 256)] + ([(TOWN, self.NC, 1)] if self.NC else [])
        zti = lambda t0: 0 if t0 < 512 else (1 if t0 < 1024 else 2)
        with ExitStack() as st:
            lzr = Ring(E, "lnz", [st.enter_context(self.nc_sbuf("lnz%d_%d" % (which, i), [128, KC, 256], F32)) for i in range(2)])
            stat = st.enter_context(self.nc_sbuf("lnstat%d" % which, [128, 3, 256], F32))
            statb = Buf("lnstat")
            vec = st.enter_context(self.nc_sbuf("lnvec%d" % which, [128, 4, KC, 2], F32))
            vecb = Buf("lnvec")
            g_ap, b_ap = self.lng_sb[:, which, :], self.lnb_sb[:, which, :]
            if which == 0:
                for sel in range(2):
                    s2, sh2 = self.mod[:, 4 * KC:5 * KC, sel], self.mod[:, 3 * KC:4 * KC, sel]
                    E.op("dve", lambda e: e.tensor_tensor(out=vec[:, 0, :, sel], in0=g_ap, in1=s2, op=ALU.mult),
                         reads=[self.cst_b, self.mod_b], writes=[vecb])
                    E.op("dve", lambda e: e.tensor_tensor(out=vec[:, 1, :, sel], in0=b_ap, in1=s2, op=ALU.mult),
                         reads=[self.cst_b, self.mod_b], writes=[vecb])
                    E.op("dve", lambda e: e.tensor_tensor(out=vec[:, 1, :, sel], in0=vec[:, 1, :, sel], in1=sh2, op=ALU.add),
                         reads=[self.mod_b, vecb], writes=[vecb])
                E.op("dve", lambda e: e.tensor_scalar(out=vec[:, 2, :, 0], in0=g_ap, scalar1=float(DN_ALPHA), scalar2=None, op0=ALU.mult),
                     reads=[self.cst_b], writes=[vecb])
                E.op("dve", lambda e: e.tensor_scalar(out=vec[:, 3, :, 0], in0=b_ap, scalar1=float(DN_ALPHA), scalar2=None, op0=ALU.mult),
                     reads=[self.cst_b], writes=[vecb])
            def ld(idx):
                t0_, tn_, _ = ln_tiles[idx]
                lz_, lzb_, s_ = lzr.next()
                E.dma("sp", s_, lz_[:, :, 0:tn_], zv[:, :, t0_:t0_ + tn_], writes=[lzb_])
                return lz_, lzb_, s_
            nxt_ld = ld(0)
            for idx, (t0, tn, sel) in enumerate(ln_tiles):
                ti = zti(t0)
                lz, lzb1, s_lz = nxt_ld
                lzb = [lzb1]
                if idx + 1 < len(ln_tiles):
                    nxt_ld = ld(idx + 1)
                b1, b2 = E.bank(), E.bank()
                p1, p2 = self.ps[:, b1, 0:tn], self.ps[:, b2, 0:tn]

                def mm1(e):
                    for k in range(KC):
                        ins = e.matmul(p1, lhsT=self.ones_f[:], rhs=lz[:, k, 0:tn], start=(k == 0), stop=(k == KC - 1))
                    return ins
                E.op("pe", mm1, reads=lzb + [self.ones_buf], writes=[E.psb[b1]])
                for k in range(KC):
                    sq, sqb, _ = self.stg.next()
                    E.op("act", lambda e: e.activation(out=sq[:, 0:tn], in_=lz[:, k, 0:tn], func=AF.Square),
                         reads=lzb, writes=[sqb])
                    E.op("pe", lambda e: e.matmul(p2, lhsT=self.ones_f[:], rhs=sq[:, 0:tn], start=(k == 0), stop=(k == KC - 1)),
                         reads=[sqb, self.ones_buf], writes=[E.psb[b2]])
                mean, msq, rstd = stat[:, 0, 0:tn], stat[:, 1, 0:tn], stat[:, 2, 0:tn]
                E.op("dve", lambda e: e.tensor_scalar(out=mean, in0=p1, scalar1=1.0 / D, scalar2=None, op0=ALU.mult),
                     reads=[E.psb[b1]], writes=[statb])
                E.op("dve", lambda e: e.tensor_tensor(out=msq, in0=mean, in1=mean, op=ALU.mult), reads=[statb], writes=[statb])
                E.op("dve", lambda e: e.scalar_tensor_tensor(out=rstd, in0=p2, scalar=1.0 / D, in1=msq, op0=ALU.mult,
                                                             op1=ALU.subtract), reads=[E.psb[b2], statb], writes=[statb])
                E.op("dve", lambda e: e.tensor_scalar(out=rstd, in0=rstd, scalar1=float(LN_EPS), scalar2=None, op0=ALU.add),
                     reads=[statb], writes=[statb])
                E.op("act", lambda e: e.activation(out=rstd, in_=rstd, func=AF.Sqrt), reads=[statb], writes=[statb])
                E.op("dve", lambda e: e.reciprocal(out=rstd, in_=rstd), reads=[statb], writes=[statb])
                kb = [Buf("lzk%d" % k) for k in range(KC)]
                for k in range(KC):
                    zk = lz[:, k, 0:tn]
                    E.op("dve", lambda e: e.tensor_tensor(out=zk, in0=zk, in1=mean, op=ALU.subtract),
                         reads=[statb] + lzb, writes=[kb[k]])
                    E.op("dve", lambda e: e.tensor_tensor(out=zk, in0=zk, in1=rstd, op=ALU.mult),
                         reads=[statb, kb[k]], writes=[kb[k]])
                    if which == 0:
                        E.op("act", lambda e: e.activation(out=self.insb[:, k, t0:t0 + tn], in_=zk, func=AF.Identity,
                                                           scale=vec[:, 0, k, sel:sel + 1], bias=vec[:, 1, k, sel:sel + 1]),
                             reads=[kb[k], vecb], writes=[self.insb_b[k]])
                        E.op("act", lambda e: e.activation(out=zk, in_=zk, func=AF.Identity, scale=vec[:, 2, k, 0:1],
                                                           bias=vec[:, 3, k, 0:1]), reads=[kb[k], vecb], writes=[kb[k]])
                    else:
                        E.op("act", lambda e: e.activation(out=zk, in_=zk, func=AF.Identity, scale=self.lng_sb[:, which, k:k + 1],
                                                           bias=self.lnb_sb[:, which, k:k + 1]), reads=[kb[k], self.cst_b], writes=[kb[k]])
                if which == 0:
                    E.dma("sp", s_lz, zv[:, :, t0:t0 + tn], lz[:, :, 0:tn], reads=kb + lzb, writes=lzb)
                else:
                    if sel == 0:
                        dst = self.xo.rearrange("(kc p) t -> p kc t", p=128)[:, :, t0:t0 + tn]
                    else:
                        dst = self.co.rearrange("(kc p) t -> p kc t", p=128)[:, :, 0:tn]
                    E.dma("sp", s_lz, dst, lz[:, :, 0:tn], reads=kb + lzb, writes=lzb)
            E.barrier()

    def exchange(self):
        E, nc, fz, li = self.E, self.nc, self.fused, self.li
        E.barrier()
        G_b = fz["G_b"]
        if not getattr(self, "NOCOLL", False):
            E.coll(lambda e: e.collective_compute("AllGather", ALU.bypass, replica_groups=[list(range(NCORES))],
                                                  ins=[fz["bnd"]], outs=[fz["G"]]), reads=[fz["bnd_b"]], writes=[G_b])
        if getattr(self, "STOPCOLL", False):
            E.barrier()
            return
        Gv = fz["G"].rearrange("(r s kc p) t -> p r s kc t", r=NCORES, s=2, p=128)
        xn = fz["xext"][li + 1]
        with ExitStack() as st:
            selt = st.enter_context(self.nc_sbuf("sel_sb", [128, 2, NCORES], F32))
            selb = Buf("sel")
            E.dma("pool", E.new_sem("s_sel"), selt[:], fz["sel"], writes=[selb])
            gl = Ring(E, "gl", [st.enter_context(self.nc_sbuf("gl%d" % i, [128, NCORES, 128], F32)) for i in range(2)])
            ga = Ring(E, "ga", [st.enter_context(self.nc_sbuf("ga%d" % i, [128, 128], F32)) for i in range(2)])
            xb = Buf("xext_halo")
            for side in range(2):
                for kc in range(KC):
                    t, b, sm = gl.next()
                    E.dma("pool", sm, t[:], Gv[:, :, 1 - side, kc, :], reads=[G_b], writes=[b])
                    a, ab, asem = ga.next()
                    E.op("dve", lambda e: e.tensor_scalar(out=a[:], in0=t[:, 0, :], scalar1=selt[:, side, 0:1], scalar2=None,
                                                          op0=ALU.mult), reads=[b, selb], writes=[ab])
                    for r in range(1, NCORES):
                        E.op("dve", lambda e: e.scalar_tensor_tensor(out=a[:], in0=t[:, r, :], scalar=selt[:, side, r:r + 1], in1=a[:],
                                                                     op0=ALU.mult, op1=ALU.add), reads=[b, selb, ab], writes=[ab])
                    col0 = 0 if side == 0 else 128 + TOWN
                    E.dma("pool", asem, xn[kc * 128:(kc + 1) * 128, col0:col0 + 128], a[:], reads=[ab], writes=[xb])
            E.barrier()

    def attention(self):
        E, nc = self.E, self.nc
        NQ = self.T
        scale = float(HD ** -0.5)
        with ExitStack() as st:
            sbt = lambda n, s, dt: st.enter_context(self.nc_sbuf(n, s, dt))
            cos, sin = sbt("cos_sb", [128, 1280], F32), sbt("sin_sb", [128, 1280], F32)
            perm = sbt("perm_sb", [128, 128], F32)
            masks = sbt("masks_sb", [128, 4, 512], BF16)
            esink = sbt("esink", [128, NH], F32)
            qg = sbt("qg", [128, 4, 1280], BF16)
            kg = sbt("kg", [128, 1536], BF16)
            vg = sbt("vg", [128, 12, 128], BF16)
            pt = [sbt("pt%d" % i, [128, 5, 512], BF16) for i in range(2)]
            rden = [sbt("rden%d" % i, [128, 512], F32) for i in range(2)]
            acst = Buf("acst")
            s_acst = E.new_sem("s_acst")
            qgb = [Buf("qg%d" % h) for h in range(4)]
            kgb, vgb = Buf("kg"), Buf("vg")
            ptb = [Buf("pt%d" % i) for i in range(2)]
            rdb = [Buf("rden%d" % i) for i in range(2)]
            for dst, src in ((cos, self.cos), (sin, self.sin), (perm, self.perm)):
                E.dma("sp", s_acst, dst[:], src, writes=[acst])
            E.dma("pool", E.new_sem("s_acst_p"), masks[:], self.masks, writes=[acst])
            E.dma("sp", s_acst, esink[:], self.sinkb, writes=[acst])
            E.op("act", lambda e: e.activation(out=esink[:], in_=esink[:], func=AF.Exp), reads=[acst], writes=[acst])

            def rope(pap, pbuf, tn, dst_ap, dst_buf, e0):
                qf, qfb, _ = self.stg.next()
                E.op("act", lambda e: e.activation(out=qf[:, 0:tn], in_=pap, func=AF.Copy), reads=[pbuf], writes=[qfb])
                bk = E.bank()
                psw = self.ps[:, bk, 0:tn]
                E.op("pe", lambda e: e.matmul(psw, lhsT=perm[:], rhs=qf[:, 0:tn], start=True, stop=True),
                     reads=[qfb, acst], writes=[E.psb[bk]])
                t2, t2b, _ = self.stg.next()
                E.op("dve", lambda e: e.tensor_tensor(out=t2[:, 0:tn], in0=psw, in1=sin[:, e0:e0 + tn], op=ALU.mult),
                     reads=[E.psb[bk], acst], writes=[t2b])
                E.op("dve", lambda e: e.tensor_tensor(out=qf[:, 0:tn], in0=qf[:, 0:tn], in1=cos[:, e0:e0 + tn], op=ALU.mult),
                     reads=[qfb, acst], writes=[qfb])
                E.op("dve", lambda e: e.tensor_tensor(out=dst_ap, in0=qf[:, 0:tn], in1=t2[:, 0:tn], op=ALU.add),
                     reads=[qfb, t2b], writes=[dst_buf])

            q_tiles = [(128, 512, 0), (640, 512, 0)] + ([(1280, NCO, 1)] if self.NC else [])
            k_tiles = [(0, 512, 0), (512, 512, 0), (1024, 256, 0), (1280, 256, 1)]
            v_tiles = [(c * 128, 128, 0) for c in range(12)]
            n_qblk = 8 + (1 if self.NC else 0)
            it = 0
            for g in range(NKV):
                def epi_q(si, ci, ti, tile, pap, pbuf):
                    t0, tn, sel = tile
                    h = si * 2 + ci
                    if sel == 0:
                        rope(pap, pbuf, tn, qg[:, h, t0 - 128:t0 - 128 + tn], qgb[h], t0)
                    else:
                        E.op("act", lambda e: e.activation(out=qg[:, h, 1024:1024 + tn], in_=pap, func=AF.Copy),
                             reads=[pbuf], writes=[qgb[h]])
                self.gemm(self.wqkv, 0, KC, [[(512 * g, 256)], [(512 * g + 256, 256)]], q_tiles, epi_q, key="q%d" % g,
                          then=("kv%d" % g, self.wqkv, KC, [(4096 + 128 * g, 128), (5120 + 128 * g, 128)]))

                def epi_kv(si, ci, ti, tile, pap, pbuf):
                    t0, tn, sel = tile
                    if ci == 0:
                        if sel == 0:
                            rope(pap, pbuf, tn, kg[:, t0:t0 + tn], kgb, t0)
                        else:
                            E.op("act", lambda e: e.activation(out=kg[:, t0:t0 + tn], in_=pap, func=AF.Copy),
                                 reads=[pbuf], writes=[kgb])
                    else:
                        E.op("act", lambda e: e.activation(out=vg[:, t0 // 128, :], in_=pap, func=AF.Copy),
                             reads=[pbuf], writes=[vgb])
                then = ("q%d" % (g + 1), self.wqkv, KC, [(512 * (g + 1), 256)]) if g + 1 < NKV else ("oproj", self.wo, KC, [(0, self.SN)])
                self.gemm(self.wqkv, 0, KC, [[(4096 + 128 * g, 128), (5120 + 128 * g, 128)]], k_tiles, epi_kv,
                          mode=lambda si, ci: "N" if ci == 0 else v_tiles, key="kv%d" % g, then=then)

                for qb in range(n_qblk):
                    if qb < 8:
                        kch = [qb, qb + 1, qb + 2, 10, 11]
                        mk = [0 if qb == 0 else 1, None, 3 if qb == 7 else 2, None, None]
                    else:
                        kch = [10, 11]
                        mk = [None, None]
                    q0 = qb * 128
                    qw = 128 if qb < 8 else NCO
                    nw = 4 * qw
                    P, Pb = pt[it % 2], ptb[it % 2]
                    R, Rb = rden[it % 2], rdb[it % 2]
                    it += 1
                    for j, kc_ in enumerate(kch):
                        bk = E.bank()
                        sp_ = self.ps[:, bk, 0:nw]
                        E.op("pe", lambda e: e.matmul(sp_, lhsT=kg[:, kc_ * 128:(kc_ + 1) * 128], rhs=qg[:, :, q0:q0 + qw],
                                                      start=True, stop=True), reads=[kgb] + qgb, writes=[E.psb[bk]])
                        E.op("act", lambda e: e.activation(out=P[:, j, 0:nw], in_=sp_, func=AF.Exp, scale=scale),
                             reads=[E.psb[bk]], writes=[Pb])
                        if mk[j] is not None:
                            E.op("dve", lambda e: e.tensor_tensor(out=P[:, j, :], in0=P[:, j, :], in1=masks[:, mk[j], :],
                                                                  op=ALU.mult), reads=[Pb, acst], writes=[Pb])
                    bd, bo = E.bank(), E.bank()
                    pd, po = self.ps[:, bd, 0:nw], self.ps[:, bo, 0:nw]

                    def mmd(e):
                        for j in range(len(kch)):
                            ins = e.matmul(pd, lhsT=self.ones_b[:], rhs=P[:, j, 0:nw], start=(j == 0), stop=(j == len(kch) - 1))
                        return ins
                    E.op("pe", mmd, reads=[Pb, self.ones_buf], writes=[E.psb[bd]])

                    def mmo(e):
                        for j, kc_ in enumerate(kch):
                            ins = e.matmul(po, lhsT=vg[:, kc_, :], rhs=P[:, j, 0:nw], start=(j == 0), stop=(j == len(kch) - 1))
                        return ins
                    E.op("pe", mmo, reads=[Pb, vgb], writes=[E.psb[bo]])
                    for h in range(4):
                        E.op("dve", lambda e: e.tensor_scalar(out=R[:, h * qw:(h + 1) * qw], in0=pd[:, h * qw:(h + 1) * qw],
                                                              scalar1=esink[:, 4 * g + h:4 * g + h + 1], scalar2=None,
                                                              op0=ALU.add), reads=[E.psb[bd], acst], writes=[Rb])
                    E.op("dve", lambda e: e.reciprocal(out=R[:, 0:nw], in_=R[:, 0:nw]), reads=[Rb], writes=[Rb])
                    ob, obf, osem = self.stgb.next()
                    E.op("dve", lambda e: e.tensor_tensor(out=ob[:, 0:nw], in0=po, in1=R[:, 0:nw], op=ALU.mult),
                         reads=[E.psb[bo], Rb], writes=[obf])
                    dst = self.obuf[g * 512:(g + 1) * 512, q0:q0 + qw].rearrange("(h p) q -> p h q", p=128)
                    E.dma("sp", osem, dst, ob[:, 0:nw].rearrange("p (h q) -> p h q", h=4), reads=[obf],
                          writes=self.ob[4 * g:4 * g + 4])
            E.barrier()
            self.out_proj(self.wo)

    def out_proj(self, W):
        E = self.E
        ov = self.obuf.rearrange("(kc p) t -> p kc t", p=128)
        E.dma("sp", self.s_insb, self.insb[:, :, 0:self.T], ov, reads=self.ob, writes=self.insb_b)

        SN = self.SN

        def epi(si, ci, ti, tile, pap, pbuf):
            n = si * (SN // 128) + ci
            self.acc_z(n, ti, tile, pap, pbuf, self.mv(2, n, tile[2]))
        self.gemm(W, 0, KC, [[(c, SN)] for c in range(0, D, SN)], self.tiles, epi, key="oproj")

    def conv_mixer(self):
        E, nc = self.E, self.nc
        XE, CE = self.XE, self.CE
        L = XE + CE
        c_tiles = [(0, 512, 0), (512, 512, 0), (1024, XE - 1024, 0), (XE, CE, 1)]
        with ExitStack() as st:
            sbt = lambda n, s, dt: st.enter_context(self.nc_sbuf(n, s, dt))
            cw, cb = sbt("cw", [128, 3, KC], F32), sbt("cb", [128, KC], F32)
            Bs = [sbt("Bs%d" % i, [128, L], F32) for i in range(2)]
            Cs = [sbt("Cs%d" % i, [128, L], F32) for i in range(2)]
            Vs = [sbt("Vs%d" % i, [128, L], F32) for i in range(2)]
            Ys = [sbt("Ys%d" % i, [128, L], F32) for i in range(2)]
            Gs = [sbt("Gs%d" % i, [128, L], BF16) for i in range(2)]
            Bb = [Buf("Bs%d" % i) for i in range(2)]
            Cb = [Buf("Cs%d" % i) for i in range(2)]
            Vb = [Buf("Vs%d" % i) for i in range(2)]
            Yb = [Buf("Ys%d" % i) for i in range(2)]
            Gb = [Buf("Gs%d" % i) for i in range(2)]
            gsem = [E.new_sem("gs%d" % i) for i in range(2)]
            ccst = Buf("ccst")
            s_c = E.new_sem("s_ccst")
            E.dma("sp", s_c, cw[:], self.convw, writes=[ccst])
            E.dma("sp", s_c, cb[:], self.convb, writes=[ccst])

            def epi(si, ci, ti, tile, pap, pbuf):
                t0, tn, sel = tile
                p = si % 2
                if ci == 0:
                    E.op("act", lambda e: e.activation(out=Bs[p][:, t0:t0 + tn], in_=pap, func=AF.Copy), reads=[pbuf], writes=[Bb[p]])
                elif ci == 1:
                    E.op("act", lambda e: e.activation(out=Cs[p][:, t0:t0 + tn], in_=pap, func=AF.Copy), reads=[pbuf], writes=[Cb[p]])
                else:
                    E.op("dve", lambda e: e.tensor_tensor(out=Vs[p][:, t0:t0 + tn], in0=pap, in1=Cs[p][:, t0:t0 + tn], op=ALU.mult),
                         reads=[pbuf, Cb[p]], writes=[Vb[p]])
                    if ti == len(c_tiles) - 1:
                        n = si
                        V, Y, B, G = Vs[p], Ys[p], Bs[p], Gs[p]
                        E.op("dve", lambda e: e.tensor_scalar(out=Y[:, 1:L - 1], in0=V[:, 0:L - 2], scalar1=cw[:, 0, n:n + 1],
                                                              scalar2=None, op0=ALU.mult), reads=[Vb[p], ccst], writes=[Yb[p]])
                        E.op("dve", lambda e: e.scalar_tensor_tensor(out=Y[:, 1:L - 1], in0=V[:, 1:L - 1], scalar=cw[:, 1, n:n + 1],
                                                                     in1=Y[:, 1:L - 1], op0=ALU.mult, op1=ALU.add),
                             reads=[Vb[p], ccst, Yb[p]], writes=[Yb[p]])
                        E.op("dve", lambda e: e.scalar_tensor_tensor(out=Y[:, 1:L - 1], in0=V[:, 2:L], scalar=cw[:, 2, n:n + 1],
                                                                     in1=Y[:, 1:L - 1], op0=ALU.mult, op1=ALU.add),
                             reads=[Vb[p], ccst, Yb[p]], writes=[Yb[p]])
                        E.op("dve", lambda e: e.scalar_tensor_tensor(out=G[:, 1:L - 1], in0=Y[:, 1:L - 1], scalar=cb[:, n:n + 1],
                                                                     in1=B[:, 1:L - 1], op0=ALU.add, op1=ALU.mult),
                             reads=[Yb[p], ccst, Bb[p]], writes=[Gb[p]])
                        E.dma("sp", gsem[p], self.obuf[n * 128:(n + 1) * 128, 0:TOWN], G[:, 1:1 + TOWN], reads=[Gb[p]], writes=[self.ob[n]])
                        if self.NC:
                            E.dma("sp", gsem[p], self.obuf[n * 128:(n + 1) * 128, TOWN:TOWN + NCO], G[:, XE + 1:XE + 1 + NCO],
                                  reads=[Gb[p]], writes=[self.ob[n]])
            slabs = [[(n * 128, 128), (D + n * 128, 128), (2 * D + n * 128, 128)] for n in range(KC)]
            self.gemm(self.win, 0, KC, slabs, c_tiles, epi, then=("oproj", self.wout, KC, [(0, self.SN)]))
            E.barrier()
            self.out_proj(self.wout)

    def prep_pool(self):
        E, nc = self.E, self.nc
        XE, CE, HL, CP = self.XE, self.CE, self.HL, self.CP
        L = XE + CE
        xv = self.xT.rearrange("(kc p) t -> p kc t", p=128)
        cv = self.cT.rearrange("(kc p) t -> p kc t", p=128)
        zv = self.z.rearrange("(kc p) t -> p kc t", p=128)
        with ExitStack() as st:
            sbt = lambda n, s, dt: st.enter_context(self.nc_sbuf(n, s, dt))
            xl = Ring(E, "xl", [sbt("xl%d" % i, [128, L], F32) for i in range(2)])
            zs = Ring(E, "zs", [sbt("zs%d" % i, [128, TOWN + NCTX], F32) for i in range(2)])
            invc = sbt("invc_sb", [128, 4, L], F32)
            U = [sbt("pU%d" % i, [128, L], F32) for i in range(2)]
            A = [sbt("pA%d" % i, [128, L], F32) for i in range(2)]
            Bq = [sbt("pB%d" % i, [128, L], F32) for i in range(2)]
            Ub = [Buf("pU%d" % i) for i in range(2)]
            Ab = [Buf("pA%d" % i) for i in range(2)]
            Bb = [Buf("pB%d" % i) for i in range(2)]
            icb = Buf("invc")
            s_ic = E.new_sem("s_invc")
            E.dma("sp", s_ic, invc[:], self.invc, writes=[icb])
            def ldx(kc_):
                t_, b_, s_ = xl.next()
                E.dma("sp", s_, t_[:, 0:XE], xv[:, kc_, :], writes=[b_])
                E.dma("sp", s_, t_[:, XE:L], cv[:, kc_, :], writes=[b_])
                return t_, b_, s_
            nx = ldx(0)
            for kc in range(KC):
                p = kc % 2
                gi = kc // 8
                t, b, s = nx
                if kc + 1 < KC:
                    nx = ldx(kc + 1)
                u, ub = U[p], Ub[p]
                E.op("act", lambda e: e.activation(out=u[:, 0:XE], in_=t[:, 0:XE], func=AF.Identity,
                                                   scale=self.mv(1, kc, 0), bias=self.mv(0, kc, 0)), reads=[b, self.mod_b], writes=[ub])
                E.op("act", lambda e: e.activation(out=u[:, XE:L], in_=t[:, XE:L], func=AF.Identity,
                                                   scale=self.mv(1, kc, 1), bias=self.mv(0, kc, 1)), reads=[b, self.mod_b], writes=[ub])
                E.op("dve", lambda e: e.tensor_scalar(out=u[:, 0:HL], in0=u[:, 0:HL], scalar1=self.edge_sb[:, 0:1], scalar2=None,
                                                      op0=ALU.mult), reads=[self.cst_b, ub], writes=[ub])
                E.op("dve", lambda e: e.tensor_scalar(out=u[:, HL + TOWN:XE], in0=u[:, HL + TOWN:XE], scalar1=self.edge_sb[:, 1:2],
                                                      scalar2=None, op0=ALU.mult), reads=[self.cst_b, ub], writes=[ub])
                E.op("dve", lambda e: e.tensor_scalar(out=u[:, XE:XE + CP], in0=u[:, XE:XE + CP], scalar1=self.edge_sb[:, 0:1],
                                                      scalar2=None, op0=ALU.mult), reads=[self.cst_b, ub], writes=[ub])
                E.op("dve", lambda e: e.tensor_scalar(out=u[:, XE + CP + NCO:L], in0=u[:, XE + CP + NCO:L], scalar1=self.edge_sb[:, 1:2],
                                                      scalar2=None, op0=ALU.mult), reads=[self.cst_b, ub], writes=[ub])
                a, ab, bq, bb = A[p], Ab[p], Bq[p], Bb[p]
                E.op("dve", lambda e: e.tensor_tensor(out=a[:, 1:L], in0=u[:, 0:L - 1], in1=u[:, 1:L], op=ALU.add),
                     reads=[ub], writes=[ab])
                cur, curb, oth, othb = a, ab, bq, bb
                lo, hi = 1, L
                for step, (dl, dr) in enumerate(((1, 1), (2, 2), (4, 4))):
                    if step >= gi:
                        break
                    nlo, nhi = lo + dl, hi - dr
                    E.op("dve", lambda e: e.tensor_tensor(out=oth[:, nlo:nhi], in0=cur[:, nlo - dl:nhi - dl],
                                                          in1=cur[:, nlo + dr:nhi + dr], op=ALU.add), reads=[curb], writes=[othb])
                    cur, curb, oth, othb = oth, othb, cur, curb
                    lo, hi = nlo, nhi
                E.op("dve", lambda e: e.tensor_tensor(out=cur[:, 8:L - 8], in0=cur[:, 8:L - 8], in1=invc[:, gi, 8:L - 8], op=ALU.mult),
                     reads=[curb, icb], writes=[curb])
                ib = self.insb_b[kc]
                E.op("dve", lambda e: e.tensor_tensor(out=self.insb[:, kc, 0:TOWN], in0=cur[:, HL:HL + TOWN], in1=u[:, HL:HL + TOWN],
                                                      op=ALU.subtract), reads=[curb, ub], writes=[ib])
                E.op("dve", lambda e: e.tensor_tensor(out=self.insb[:, kc, TOWN:TOWN + NCO], in0=cur[:, XE + CP:XE + CP + NCO],
                                                      in1=u[:, XE + CP:XE + CP + NCO], op=ALU.subtract), reads=[curb, ub], writes=[ib])
                zt, zb_, zsem = zs.next()
                E.op("act", lambda e: e.activation(out=zt[:, 0:TOWN], in_=t[:, HL:HL + TOWN], func=AF.Copy, scale=float(DN_ALPHA)),
                     reads=[b], writes=[zb_])
                E.op("act", lambda e: e.activation(out=zt[:, TOWN:TOWN + NCO], in_=t[:, XE + CP:XE + CP + NCO], func=AF.Copy,
                                                   scale=float(DN_ALPHA)), reads=[b], writes=[zb_])
                E.dma("sp", zsem, zv[:, kc, :], zt[:, 0:self.T], reads=[zb_], writes=self.zb[kc])
            E.barrier()

    def pool_mixer(self):
        E, nc = self.E, self.nc
        with ExitStack() as st:
            psc = st.enter_context(self.nc_sbuf("psc", [128, KC], F32))
            pg = st.enter_context(self.nc_sbuf("pg", [128, KC, 2], F32))
            pgb = Buf("pg")
            s_p = E.new_sem("s_psc")
            E.dma("sp", s_p, psc[:], self.pscale, writes=[pgb])
            for sel in range(2):
                E.op("dve", lambda e: e.tensor_tensor(out=pg[:, :, sel], in0=self.mod[:, 2 * KC:3 * KC, sel], in1=psc[:], op=ALU.mult),
                     reads=[pgb, self.mod_b], writes=[pgb])
            for gi in range(4):
                def epi(si, ci, ti, tile, pap, pbuf):
                    n = gi * 8 + si * 4 + ci
                    self.acc_z(n, ti, tile, pap, pbuf, pg[:, n, tile[2]:tile[2] + 1], extra_reads=[pgb])
                self.gemm(self.poolw[gi], gi * 8, 8, [[(c, 512)] for c in range(0, 1024, 512)], self.tiles, epi, key="p%d" % gi,
                          then=(("p%d" % (gi + 1), self.poolw[gi + 1], 8, [(0, 512)]) if gi < 3 else None))
            E.barrier()


def _pm(v):
    v = np.asarray(v, np.float32)
    lead = v.shape[:-1]
    r = v.reshape(lead + (KC, 128))
    return np.ascontiguousarray(np.moveaxis(r, -1, 0))


def _rope_tables(core):
    e = np.arange(1280)
    t = core * TOWN - 128 + e
    row = (t // 64).astype(np.float32)
    col = (t % 64).astype(np.float32)
    freqs = np.power(np.float32(10000.0), -np.arange(32, dtype=np.float32) / np.float32(32)).astype(np.float32)
    ang = np.stack([row[:, None] * freqs[None, :], col[:, None] * freqs[None, :]], axis=0).astype(np.float32)
    cosv, sinv = np.cos(ang).astype(np.float32), np.sin(ang).astype(np.float32)
    cos = np.zeros((128, 1280), np.float32)
    sin = np.zeros((128, 1280), np.float32)
    for a in range(2):
        for b in range(2):
            cos[a * 64 + b * 32:a * 64 + b * 32 + 32, :] = cosv[a].T
            sin[a * 64 + b * 32:a * 64 + b * 32 + 32, :] = sinv[a].T
    return cos, sin


def _perm():
    p = np.zeros((128, 128), np.float32)
    for a in range(2):
        for f in range(32):
            p[a * 64 + 32 + f, a * 64 + f] = -1.0
            p[a * 64 + f, a * 64 + 32 + f] = 1.0
    return p


def _masks(core):
    j = np.arange(128)[:, None]
    q = np.arange(128)[None, :]
    mp = (j >= q).astype(np.float32)
    mn = (j <= q).astype(np.float32)
    m = np.stack([mp * (1.0 if core > 0 else 0.0), mp, mn, mn * (1.0 if core < NCORES - 1 else 0.0)], axis=1)
    return np.ascontiguousarray(np.broadcast_to(m[:, :, None, :], (128, 4, 4, 128)).reshape(128, 4, 512))


def _invc(core, XE, CE):
    L = XE + CE
    out = np.ones((4, L), np.float32)
    for gi, w in enumerate(POOL_WINDOWS):
        for (off, n, S, t0) in ((0, XE, SEQ, core * TOWN - 8), (XE, CE, NCTX, core * NCO - 8)):
            t = t0 + np.arange(n)
            lo = np.clip(t - w // 2, 0, S)
            hi = np.clip(t + w - w // 2, 0, S)
            cnt = (hi - lo).astype(np.float32)
            ok = (t >= 0) & (t < S)
            out[gi, off:off + n] = np.where(ok, np.float32(1.0) / np.maximum(cnt, 1.0), np.float32(1.0))
    return np.ascontiguousarray(np.broadcast_to(out[None], (128, 4, L)))


_PROGS = {}


def _get_prog(kind, last):
    key = (kind, last)
    if key not in _PROGS:
        nc = bass.Bass("TRN2", target_bir_lowering=False)
        LayerProg(nc, kind, last).build()
        _PROGS[key] = nc
    return _PROGS[key]


def _layer_inputs(i, x, ctx, inp, sfx="_L0", with_x=True):
    kind, j = KINDS[i], i // 3
    last = i == DEPTH - 1
    HL, CP = HALO[kind], CPAD[kind]
    XE, CE = TOWN + 2 * HL, (NCTX if kind == 0 else NCO + 2 * CP)
    f = lambda a: np.ascontiguousarray(np.asarray(a, np.float32))
    xp = np.pad(x, ((HL, HL), (0, 0)))
    cpad = np.pad(ctx, ((CP, CP), (0, 0)))
    cc = np.ascontiguousarray(np.stack([_pm(inp["c"][0]), _pm(inp["c_ctx"])], axis=-1))
    common = {
        "cc": cc,
        "w_down": f(inp["mod_w_down"][i]), "w_up": f(inp["mod_w_up"][i]),
        "modb": _pm(inp["mod_b"][i].reshape(6, D)).reshape(128, 6 * KC),
        "lng": _pm(inp["ln_g"][i]), "lnb": _pm(inp["ln_b"][i]),
        "w1": f(inp["mlp_w1"][i]), "w2": f(inp["mlp_w2"][i]),
    }
    if kind == 0:
        common.update({"wqkv": f(inp["attn_w_qkv"][j]), "wo": f(inp["attn_w_o"][j]),
                       "sinkb": np.ascontiguousarray(np.broadcast_to(np.asarray(inp["attn_sink"][j], np.float32)[None, :], (128, NH))),
                       "perm": _perm()})
    elif kind == 1:
        common.update({"poolw": f(inp["pool_w"][j]), "pscale": _pm(inp["pool_scale"][j])})
    else:
        common.update({"win": f(inp["conv_w_in"][j]), "convw": _pm(inp["conv_w"][j]), "convb": _pm(inp["conv_b"][j]),
                       "wout": f(inp["conv_w_out"][j])})
    maps = []
    for c in range(NCORES):
        m = dict(common)
        if with_x:
            m["xT"] = np.ascontiguousarray(xp[c * TOWN:c * TOWN + XE].T)
            if kind == 0:
                m["cT"] = np.ascontiguousarray(np.roll(ctx, -NCO * c, axis=0).T)
            else:
                m["cT"] = np.ascontiguousarray(cpad[c * NCO:c * NCO + CE].T)
        m["edge"] = np.ascontiguousarray(np.broadcast_to(
            np.array([[1.0 if c > 0 else 0.0, 1.0 if c < NCORES - 1 else 0.0]], np.float32), (128, 2)))
        if kind == 0:
            m["cos"], m["sin"] = _rope_tables(c)
            m["masks"] = _masks(c)
        elif kind == 1:
            m["invc"] = _invc(c, XE, CE)
        maps.append({k + sfx: v for k, v in m.items()})
    return maps


def run_layer(i, x, ctx, inp):
    kind = KINDS[i]
    last = i == DEPTH - 1
    nc = _get_prog(kind, last)
    maps = _layer_inputs(i, x, ctx, inp)
    res = run_bass_kernel_spmd(nc, maps, core_ids=list(range(NCORES)))
    xn = np.concatenate([np.asarray(r["xo_L0"]).T for r in res.results], axis=0)
    cn = None if last else np.concatenate([np.asarray(r["co_L0"]).T for r in res.results], axis=0)
    return xn, cn


def build_mega():
    nc = bass.Bass("TRN2", target_bir_lowering=False)
    with ExitStack() as st0:
        E = Env(nc, st0)
        itn = lambda n, sh, dt=F32: nc.dram_tensor(n, sh, dt, kind="Internal").ap()
        fz = {"xext": {li: itn("xext%d" % li, [D, TOWN + 256]) for li in range(1, DEPTH)},
              "cext": {li: itn("cext%d" % li, [D, NCTX + 16]) for li in range(1, DEPTH)},
              "bnd": itn("bnd", [2 * D, 128]), "G": itn("Gall", [NCORES * 2 * D, 128]),
              "sel": nc.dram_tensor("sel", [128, 2, NCORES], F32, kind="ExternalInput").ap(),
              "bnd_b": Buf("bnd"), "G_b": Buf("G")}
        for li in range(DEPTH):
            lp = LayerProg(nc, KINDS[li], li == DEPTH - 1, li, E, fz)
            lp.build()
            if li < DEPTH - 1:
                lp.exchange()
        E.barrier()
    return nc


_MEGA = []


def kernel_fused(inp):
    if not _MEGA:
        _MEGA.append(build_mega())
    nc = _MEGA[0]
    x = np.ascontiguousarray(inp["x"][0], dtype=np.float32)
    ctx = np.ascontiguousarray(inp["ctx"][0], dtype=np.float32)
    maps = [dict() for _ in range(NCORES)]
    for i in range(DEPTH):
        lm = _layer_inputs(i, x, ctx, inp, sfx="_L%d" % i, with_x=(i == 0))
        for c in range(NCORES):
            maps[c].update(lm[c])
    for c in range(NCORES):
        selv = np.zeros((2, NCORES), np.float32)
        if c > 0:
            selv[0, c - 1] = 1.0
        if c < NCORES - 1:
            selv[1, c + 1] = 1.0
        maps[c]["sel"] = np.ascontiguousarray(np.broadcast_to(selv[None], (128, 2, NCORES)))
    res = run_bass_kernel_spmd(nc, maps, core_ids=list(range(NCORES)))
    xn = np.concatenate([np.asarray(r["xo_L%d" % (DEPTH - 1)]).T for r in res.results], axis=0)
    return xn[None].astype(np.float32)


def kernel_unfused(inp):
    x = np.ascontiguousarray(inp["x"][0], dtype=np.float32)
    ctx = np.ascontiguousarray(inp["ctx"][0], dtype=np.float32)
    for i in range(DEPTH):
        x, ctx = run_layer(i, x, ctx, inp)
    return x[None].astype(np.float32)


FUSED = False


def kernel(**inputs):
    inp = {k: np.asarray(v) for k, v in inputs.items()}
    return kernel_fused(inp) if FUSED else kernel_unfused(inp)
```

```python
import numpy as np
from contextlib import ExitStack
import concourse.bass as bass
import concourse.mybir as mybir
from concourse.bass_utils import run_bass_kernel_spmd

F32 = mybir.dt.float32
BF16 = mybir.dt.bfloat16
AF = mybir.ActivationFunctionType
ALU = mybir.AluOpType

D = 4096
KC = 32
SEQ = 8192
NCORES = 8
TOWN = SEQ // NCORES
NCTX = 256
DFF = 4 * D
DEPTH = 4
HD = 128
NH = 32
NKV = 8
RANK = 1024
DN_ALPHA = (2 * DEPTH) ** 0.25
LN_EPS = 1e-5
POOL_WINDOWS = (2, 4, 8, 16)
KINDS = [0, 1, 2, 0]
HALO = {0: 128, 1: 8, 2: 1}
CPAD = {0: 0, 1: 8, 2: 1}
NCO = 32


class Sem:
    def __init__(self, h, is_dma):
        self.h = h
        self.n = 0
        self.is_dma = is_dma


class Buf:
    __slots__ = ("name", "w", "r")

    def __init__(self, name):
        self.name = name
        self.w = None
        self.r = {}


class Env:
    def __init__(self, nc, stack):
        self.nc = nc
        self.stack = stack
        self.eng = {"pe": nc.tensor, "act": nc.scalar, "dve": nc.vector, "pool": nc.gpsimd, "sp": nc.sync}
        self.sems = []
        self.semd = {}
        self.esem = {k: self.new_sem("e_" + k, False) for k in ("pe", "act", "dve", "pool")}
        self.waited = {k: {} for k in self.eng}
        self.nbank = 0
        self.psb = [Buf("ps%d" % i) for i in range(8)]

    def new_sem(self, name, is_dma=True):
        if name in self.semd:
            return self.semd[name]
        h = self.stack.enter_context(self.nc.semaphore(name))
        s = Sem(h, is_dma)
        s.name = name
        self.sems.append(s)
        self.semd[name] = s
        return s

    def coll(self, fn, reads=(), writes=()):
        self._deps("pool", reads, writes)
        sem = self.new_sem("s_coll", False)
        ins = fn(self.eng["pool"])
        ins.then_inc(sem.h, 1)
        sem.n += 1
        ev = (sem, sem.n)
        self.wait("pool", ev)
        self._upd(ev, reads, writes)
        return ev

    def wait(self, en, ev):
        if ev is None:
            return
        sem, val = ev
        if sem.is_dma:
            val = sem.n
        w = self.waited[en]
        if w.get(sem, 0) >= val:
            return
        self.eng[en].wait_ge(sem.h, val)
        w[sem] = val

    def _deps(self, en, reads, writes):
        for b in reads:
            self.wait(en, b.w)
        for b in writes:
            self.wait(en, b.w)
            for s, v in b.r.items():
                self.wait(en, (s, v))

    def _upd(self, ev, reads, writes):
        s, v = ev
        for b in reads:
            if b.r.get(s, 0) < v:
                b.r[s] = v
        for b in writes:
            b.w = ev
            b.r = {}

    def op(self, en, fn, reads=(), writes=()):
        self._deps(en, reads, writes)
        ins = fn(self.eng[en])
        sem = self.esem[en]
        ins.then_inc(sem.h, 1)
        sem.n += 1
        ev = (sem, sem.n)
        self._upd(ev, reads, writes)
        return ev

    def dma(self, q, sem, out, in_, reads=(), writes=(), accum=False):
        self._deps(q, reads, writes)
        if accum:
            ins = self.eng[q].dma_start(out=out, in_=in_, accum_op=ALU.add)
        else:
            ins = self.eng[q].dma_start(out=out, in_=in_)
        ins.then_inc(sem.h, 16)
        sem.n += 16
        ev = (sem, sem.n)
        self._upd(ev, reads, writes)
        return ev

    def bank(self):
        b = self.nbank % 8
        self.nbank += 1
        return b

    def barrier(self):
        for en in self.eng:
            for s in self.sems:
                if s.name == "s_coll" and en != "pool":
                    continue
                if s.n > 0:
                    self.wait(en, (s, s.n))


class Ring:
    def __init__(self, E, name, tens):
        self.t = tens
        self.n = len(tens)
        self.b = [Buf("%s%d" % (name, i)) for i in range(self.n)]
        self.s = [E.new_sem("%s%d" % (name, i)) for i in range(self.n)]
        self.i = 0

    def next(self):
        i = self.i % self.n
        self.i += 1
        return self.t[i], self.b[i], self.s[i]


def chunks_of(total, step):
    out = []
    c = 0
    while c < total:
        n = min(step, total - c)
        out.append((c, n))
        c += n
    return out


class LayerProg:
    def __init__(self, nc, kind, last, li=0, E=None, fused=None, mod_in=False, make_next=False):
        self.nc = nc
        self.kind = kind
        self.last = last
        self.li = li
        self.sfx = "_L%d" % li
        self.Eshared = E
        self.fused = fused
        self.mod_in = mod_in
        self.make_next = make_next
        self.nc_sbuf = lambda n, sh, dt: nc.sbuf_tensor(n + self.sfx, sh, dt)
        self.HL = HALO[kind]
        self.CP = CPAD[kind]
        self.XE = TOWN + 2 * self.HL
        self.CE = NCTX if kind == 0 else NCO + 2 * self.CP
        self.NC = 0 if last else NCO
        self.T = TOWN + self.NC
        self.tiles = [(0, 512, 0), (512, 512, 0)] + ([(1024, NCO, 1)] if self.NC else [])

    def declare(self):
        nc, kind = self.nc, self.kind
        sfx, fz, li = self.sfx, self.fused, self.li
        ein = lambda n, s, dt=F32: nc.dram_tensor(n + sfx, s, dt, kind="ExternalInput").ap()
        if fz is not None and li > 0:
            self.xT = fz["xext"][li][:, 128 - self.HL:128 + TOWN + self.HL]
            self.cT = fz["cext"][li][:, 8 - self.CP:8 + NCTX + self.CP]
        else:
            self.xT = ein("xT", [D, self.XE])
            self.cT = ein("cT", [D, self.CE])
        self.cc = ein("cc", [128, KC, 2])
        self.edge = ein("edge", [128, 2])
        if self.mod_in:
            self.modin = ein("modin", [128, 6 * KC, 2])
        else:
            self.w_down = ein("w_down", [D, RANK])
            self.w_up = ein("w_up", [RANK, 6 * D])
            self.modb = ein("modb", [128, 6 * KC])
        if self.make_next:
            self.w_down_n = ein("w_down_n", [D, RANK])
            self.w_up_n = ein("w_up_n", [RANK, 6 * D // NCORES])
            self.modb_n = ein("modb_n", [128, 6 * KC // NCORES])
            self.modn = nc.dram_tensor("modn" + sfx, [128, 6 * KC // NCORES, 2], F32, kind="ExternalOutput").ap()
        self.lng = ein("lng", [128, 2, KC])
        self.lnb = ein("lnb", [128, 2, KC])
        self.w1 = ein("w1", [D, DFF])
        self.w2 = ein("w2", [DFF, D])
        if kind == 0:
            self.wqkv = ein("wqkv", [D, 6144])
            self.wo = ein("wo", [D, D])
            self.sinkb = ein("sinkb", [128, NH])
            self.cos = ein("cos", [128, 1280])
            self.sin = ein("sin", [128, 1280])
            self.perm = ein("perm", [128, 128])
            self.masks = ein("masks", [128, 4, 512])
        elif kind == 1:
            self.poolw = ein("poolw", [4, 1024, 1024])
            self.pscale = ein("pscale", [128, KC])
            self.invc = ein("invc", [128, 4, self.XE + self.CE])
        else:
            self.win = ein("win", [D, 3 * D])
            self.convw = ein("convw", [128, 3, KC])
            self.convb = ein("convb", [128, KC])
            self.wout = ein("wout", [D, D])
        if fz is not None and not self.last:
            self.xo = fz["xext"][li + 1][:, 128:128 + TOWN]
            self.co = fz["cext"][li + 1][:, 8:8 + NCTX]
        else:
            self.xo = nc.dram_tensor("xo" + sfx, [D, TOWN], F32, kind="ExternalOutput").ap()
            if self.NC:
                self.co = nc.dram_tensor("co" + sfx, [D, NCO], F32, kind="ExternalOutput").ap()
        self.z = nc.dram_tensor("z" + sfx, [D, self.T], F32, kind="Internal").ap()
        self.hid = nc.dram_tensor("hid" + sfx, [DFF, self.T], BF16, kind="Internal").ap()
        if kind != 1:
            self.obuf = nc.dram_tensor("obuf" + sfx, [D, self.T], BF16, kind="Internal").ap()

    def build(self):
        nc = self.nc
        self.declare()
        with ExitStack() as st:
            E = self.E = self.Eshared if self.Eshared is not None else Env(nc, st)
            sb = lambda n, s, dt: st.enter_context(self.nc_sbuf(n, s, dt))
            self.ICOLS = {0: 1536, 1: 1280, 2: 1284}[self.kind]
            self.insb = sb("insb", [128, KC, self.ICOLS], BF16)
            self.insb_b = [Buf("insb%d" % k) for k in range(KC)]
            self.s_insb = E.new_sem("s_insb")
            self.ps = st.enter_context(nc.psum_tensor("ps" + self.sfx, [128, 8, 512], F32))
            self.stg = Ring(E, "stg", [sb("stg%d" % i, [128, 512], F32) for i in range(4)])
            self.stgb = Ring(E, "stgb", [sb("stgb%d" % i, [128, 512], BF16) for i in range(4)])
            self.mod = sb("mod_sb", [128, 6 * KC, 2], F32)
            self.mod_b = Buf("mod")
            self.modb_sb = sb("modb_sb", [128, 6 * KC], F32)
            self.lng_sb = sb("lng_sb", [128, 2, KC], F32)
            self.lnb_sb = sb("lnb_sb", [128, 2, KC], F32)
            self.edge_sb = sb("edge_sb", [128, 2], F32)
            self.ones_f = sb("ones_f", [128, 128], F32)
            self.ones_b = sb("ones_b", [128, 128], BF16)
            self.cst_b = Buf("cst")
            self.s_cst = E.new_sem("s_cst")
            self.zb = [[Buf("z%d_%d" % (k, t)) for t in range(3)] for k in range(KC)]
            self.hidb = [Buf("hid%d" % m) for m in range(DFF // 128)]
            self.ob = [Buf("ob%d" % k) for k in range(KC)]
            def alloc_wsl(s2, sn=512):
                self._wn = getattr(self, "_wn", 0) + 1
                self.SN = sn
                self._pref = None
                wt = [s2.enter_context(self.nc_sbuf("wsl%d_%d" % (self._wn, i), [128, KC, sn], BF16)) for i in range(2)]
                self.wsl = Ring(E, "wsl", wt)

            with ExitStack() as s2:
                alloc_wsl(s2)
                self.load_consts()
                self.adaln()
                E.barrier()
            if self.kind == 1:
                self.prep_pool()
            else:
                self.prep()
            E.barrier()
            with ExitStack() as s2:
                alloc_wsl(s2, 256 if self.kind == 0 else 512)
                if self.kind == 0:
                    self.attention()
                elif self.kind == 1:
                    self.pool_mixer()
                else:
                    self.conv_mixer()
                E.barrier()
            self.layernorm(0)
            E.barrier()
            with ExitStack() as s2:
                alloc_wsl(s2)
                self.mlp()
                E.barrier()
            self.layernorm(1)
            E.barrier()
        return nc

    def mv(self, j, kc, sel):
        return self.mod[:, j * KC + kc, sel:sel + 1]

    def load_consts(self):
        E, nc = self.E, self.nc
        lst = [(self.lng_sb, self.lng), (self.lnb_sb, self.lnb), (self.edge_sb, self.edge)]
        if not self.mod_in:
            lst.append((self.modb_sb, self.modb))
        for dst, src in lst:
            E.dma("sp", self.s_cst, dst[:], src, writes=[self.cst_b])
        self.ones_buf = Buf("ones")
        E.op("dve", lambda e: e.memset(self.ones_f[:], 1.0), writes=[self.ones_buf])
        E.op("dve", lambda e: e.memset(self.ones_b[:], 1.0), writes=[self.ones_buf])

    def _load_slab(self, W, kcn, pieces):
        E = self.E
        Wv = W.rearrange("(kc p) n -> p kc n", p=128)
        t, b, s = self.wsl.next()
        off = 0
        for (c0, n) in pieces:
            E.dma("pool", s, t[:, 0:kcn, off:off + n], Wv[:, :, c0:c0 + n], writes=[b])
            off += n
        return t, b

    def gemm(self, W, kc0, kcn, slabs, tiles, epi, in_bufs=None, insb=None, mode=None, key=None, then=None):
        E = self.E
        insb = self.insb if insb is None else insb
        in_bufs = self.insb_b[kc0:kc0 + kcn] if in_bufs is None else in_bufs
        pf = self._pref
        if pf is not None and key is not None and pf[0] == key:
            nxt = pf[1]
        else:
            nxt = self._load_slab(W, kcn, slabs[0])
        self._pref = None
        for si in range(len(slabs)):
            wt, wb = nxt
            if si + 1 < len(slabs):
                nxt = self._load_slab(W, kcn, slabs[si + 1])
            elif then is not None:
                k2, W2, kcn2, pieces2 = then
                self._pref = (k2, self._load_slab(W2, kcn2, pieces2))
            ncol = sum(n for _, n in slabs[si])
            for ci in range(ncol // 128):
                md = mode(si, ci) if mode else "N"
                if md == "N":
                    for ti, (t0, tn, sel) in enumerate(tiles):
                        bk = E.bank()
                        pap = self.ps[:, bk, 0:tn]

                        def mm(e, pap=pap, ci=ci, t0=t0, tn=tn):
                            for k in range(kcn):
                                ins = e.matmul(pap, lhsT=wt[:, k, ci * 128:(ci + 1) * 128],
                                               rhs=insb[:, kc0 + k, t0:t0 + tn],
                                               start=(k == 0), stop=(k == kcn - 1))
                            return ins
                        E.op("pe", mm, reads=[wb] + in_bufs, writes=[E.psb[bk]])
                        epi(si, ci, ti, (t0, tn, sel), pap, E.psb[bk])
                else:
                    for ti, (t0, tn, sel) in enumerate(md):
                        bk = E.bank()
                        pap = self.ps[:, bk, 0:128]

                        def mm(e, pap=pap, ci=ci, t0=t0):
                            for k in range(kcn):
                                ins = e.matmul(pap, lhsT=insb[:, kc0 + k, t0:t0 + 128],
                                               rhs=wt[:, k, ci * 128:(ci + 1) * 128],
                                               start=(k == 0), stop=(k == kcn - 1))
                            return ins
                        E.op("pe", mm, reads=[wb] + in_bufs, writes=[E.psb[bk]])
                        epi(si, ci, ti, (t0, 128, sel), pap, E.psb[bk])

    def adaln(self):
        E, nc = self.E, self.nc
        with ExitStack() as st:
            ccs = st.enter_context(self.nc_sbuf("ccs", [128, KC, 2], F32))
            ccb = Buf("ccs")
            if (not self.mod_in) or self.make_next:
                E.dma("sp", E.new_sem("s_ccs"), ccs[:], self.cc, writes=[ccb])
                E.op("act", lambda e: e.activation(out=self.insb[:, :, 0:2], in_=ccs[:], func=AF.Silu),
                     reads=[ccb], writes=self.insb_b)
            if self.mod_in:
                E.dma("sp", E.new_sem("s_modin"), self.mod[:], self.modin, writes=[self.mod_b])
            else:
                self.adaln_core(st, "a", self.w_down, self.w_up, 6 * KC, self.modb_sb, self.cst_b, self.mod, self.mod_b)
            if self.make_next:
                nn = 6 * KC // NCORES
                mbn = st.enter_context(self.nc_sbuf("modb_n_sb", [128, nn], F32))
                mn = st.enter_context(self.nc_sbuf("modn_sb", [128, nn, 2], F32))
                mbnb, mnb = Buf("modb_n"), Buf("modn")
                E.dma("sp", E.new_sem("s_mbn"), mbn[:], self.modb_n, writes=[mbnb])
                self.adaln_core(st, "n", self.w_down_n, self.w_up_n, nn, mbn, mbnb, mn, mnb)
                E.dma("sp", E.new_sem("s_modn"), self.modn, mn[:], reads=[mnb])
            for j in (1, 4):
                E.op("dve", lambda e, j=j: e.tensor_scalar(out=self.mod[:, j * KC:(j + 1) * KC, :],
                                                           in0=self.mod[:, j * KC:(j + 1) * KC, :], scalar1=1.0,
                                                           scalar2=None, op0=ALU.add),
                     reads=[self.mod_b], writes=[self.mod_b])
            E.barrier()

    def adaln_core(self, st, tag, w_down, w_up, nchunk, bias_sb, bias_buf, out_sb, out_buf):
        E = self.E
        t1 = st.enter_context(self.nc_sbuf("t1" + tag, [128, 8, 2], BF16))
        t1b = [Buf("t1%s_%d" % (tag, i)) for i in range(8)]

        def epi_down(si, ci, ti, tile, pap, pbuf):
            n = si * 4 + ci
            E.op("dve", lambda e: e.tensor_copy(out=t1[:, n, :], in_=pap), reads=[pbuf], writes=[t1b[n]])
        self.gemm(w_down, 0, KC, [[(c, 512)] for c in range(0, RANK, 512)], [(0, 2, 0)], epi_down,
                  key="down" + tag, then=("up" + tag, w_up, 8, [(0, 512)]))

        def epi_up(si, ci, ti, tile, pap, pbuf):
            n = si * 4 + ci
            E.op("dve", lambda e: e.tensor_scalar(out=out_sb[:, n, :], in0=pap, scalar1=bias_sb[:, n:n + 1],
                                                  scalar2=None, op0=ALU.add),
                 reads=[pbuf, bias_buf], writes=[out_buf])
        self.gemm(w_up, 0, 8, [[(c, 512)] for c in range(0, nchunk * 128, 512)], [(0, 2, 0)], epi_up,
                  in_bufs=t1b, insb=t1, key="up" + tag)

    def prep(self):
        E, nc = self.E, self.nc
        XE, CE, HL, CP = self.XE, self.CE, self.HL, self.CP
        xv = self.xT.rearrange("(kc p) t -> p kc t", p=128)
        cv = self.cT.rearrange("(kc p) t -> p kc t", p=128)
        zv = self.z.rearrange("(kc p) t -> p kc t", p=128)
        with ExitStack() as st:
            xl = Ring(E, "xl", [st.enter_context(self.nc_sbuf("xl%d" % i, [128, XE + CE], F32)) for i in range(2)])
            zs = Ring(E, "zs", [st.enter_context(self.nc_sbuf("zs%d" % i, [128, TOWN + NCTX], F32)) for i in range(2)])
            def ldx(kc_):
                t_, b_, s_ = xl.next()
                E.dma("sp", s_, t_[:, 0:XE], xv[:, kc_, :], writes=[b_])
                E.dma("sp", s_, t_[:, XE:XE + CE], cv[:, kc_, :], writes=[b_])
                return t_, b_, s_
            nx = ldx(0)
            for kc in range(KC):
                t, b, s = nx
                if kc + 1 < KC:
                    nx = ldx(kc + 1)
                ib = self.insb_b[kc]
                E.op("act", lambda e: e.activation(out=self.insb[:, kc, 0:XE], in_=t[:, 0:XE], func=AF.Identity,
                                                   scale=self.mv(1, kc, 0), bias=self.mv(0, kc, 0)),
                     reads=[b, self.mod_b], writes=[ib])
                E.op("act", lambda e: e.activation(out=self.insb[:, kc, XE:XE + CE], in_=t[:, XE:XE + CE],
                                                   func=AF.Identity, scale=self.mv(1, kc, 1), bias=self.mv(0, kc, 1)),
                     reads=[b, self.mod_b], writes=[ib])
                E.op("dve", lambda e: e.tensor_scalar(out=self.insb[:, kc, 0:HL], in0=self.insb[:, kc, 0:HL],
                                                      scalar1=self.edge_sb[:, 0:1], scalar2=None, op0=ALU.mult),
                     reads=[self.cst_b], writes=[ib])
                E.op("dve", lambda e: e.tensor_scalar(out=self.insb[:, kc, HL + TOWN:XE], in0=self.insb[:, kc, HL + TOWN:XE],
                                                      scalar1=self.edge_sb[:, 1:2], scalar2=None, op0=ALU.mult),
                     reads=[self.cst_b], writes=[ib])
                if CP:
                    E.op("dve", lambda e: e.tensor_scalar(out=self.insb[:, kc, XE:XE + CP], in0=self.insb[:, kc, XE:XE + CP],
                                                          scalar1=self.edge_sb[:, 0:1], scalar2=None, op0=ALU.mult),
                         reads=[self.cst_b], writes=[ib])
                    E.op("dve", lambda e: e.tensor_scalar(out=self.insb[:, kc, XE + CP + NCO:XE + CE],
                                                          in0=self.insb[:, kc, XE + CP + NCO:XE + CE],
                                                          scalar1=self.edge_sb[:, 1:2], scalar2=None, op0=ALU.mult),
                         reads=[self.cst_b], writes=[ib])
                zt, zb_, zsem = zs.next()
                E.op("dve", lambda e: e.tensor_scalar(out=zt[:, 0:TOWN], in0=t[:, HL:HL + TOWN], scalar1=float(DN_ALPHA),
                                                      scalar2=None, op0=ALU.mult), reads=[b], writes=[zb_])
                if self.NC:
                    E.op("dve", lambda e: e.tensor_scalar(out=zt[:, TOWN:TOWN + NCO], in0=t[:, XE + CP:XE + CP + NCO],
                                                          scalar1=float(DN_ALPHA), scalar2=None, op0=ALU.mult),
                         reads=[b], writes=[zb_])
                E.dma("sp", zsem, zv[:, kc, :], zt[:, 0:self.T], reads=[zb_], writes=self.zb[kc])
            E.barrier()

    def acc_z(self, kc, ti, tile, pap, pbuf, gate_ap, extra_reads=()):
        E = self.E
        t0, tn, sel = tile
        st, sbf, ssem = self.stg.next()
        en = "act" if (self._acc_i % 2 == 0) else "dve"
        self._acc_i += 1
        if en == "act":
            E.op("act", lambda e: e.activation(out=st[:, 0:tn], in_=pap, func=AF.Identity, scale=gate_ap),
                 reads=[pbuf, self.mod_b] + list(extra_reads), writes=[sbf])
        else:
            E.op("dve", lambda e: e.tensor_scalar(out=st[:, 0:tn], in0=pap, scalar1=gate_ap, scalar2=None, op0=ALU.mult),
                 reads=[pbuf, self.mod_b] + list(extra_reads), writes=[sbf])
        zv = self.z[kc * 128:(kc + 1) * 128, t0:t0 + tn]
        E.dma("pool", ssem, zv, st[:, 0:tn], reads=[sbf], writes=[self.zb[kc][ti]], accum=True)

    _acc_i = 0
    _pref = None

    def mlp(self):
        E, nc = self.E, self.nc
        T = self.T
        hidv = self.hid.rearrange("(m p) t -> p m t", p=128)

        def epi1(si, ci, ti, tile, pap, pbuf):
            t0, tn, sel = tile
            m = si * 4 + ci
            st, sbf, ssem = self.stg.next()
            E.op("act", lambda e: e.activation(out=st[:, 0:tn], in_=pap, func=AF.Relu), reads=[pbuf], writes=[sbf])
            ob, obf, osem = self.stgb.next()
            E.op("dve", lambda e: e.tensor_tensor(out=ob[:, 0:tn], in0=st[:, 0:tn], in1=st[:, 0:tn], op=ALU.mult),
                 reads=[sbf], writes=[obf])
            E.dma("sp", osem, self.hid[m * 128:(m + 1) * 128, t0:t0 + tn], ob[:, 0:tn], reads=[obf], writes=[self.hidb[m]])
        self.gemm(self.w1, 0, KC, [[(c, 512)] for c in range(0, DFF, 512)], self.tiles, epi1,
                  then=("w2_0", self.w2[0:16 * 128, :], 16, [(0, 512)]))
        E.barrier()
        KB = 16
        nblk = DFF // (128 * KB)

        def load_blk(b):
            half = (b % 2) * KB
            E.dma("sp", E.new_sem("s_insb%d" % (b % 2)), self.insb[:, half:half + KB, 0:T], hidv[:, b * KB:(b + 1) * KB, :],
                  reads=self.hidb[b * KB:(b + 1) * KB], writes=self.insb_b[half:half + KB])
        load_blk(0)
        for b in range(nblk):
            if b + 1 < nblk:
                load_blk(b + 1)
            half = (b % 2) * KB

            def epi2(si, ci, ti, tile, pap, pbuf):
                n = si * 4 + ci
                self.acc_z(n, ti, tile, pap, pbuf, self.mv(5, n, tile[2]))
            then = ("w2_%d" % (b + 1), self.w2[(b + 1) * KB * 128:(b + 2) * KB * 128, :], KB, [(0, 512)]) if b + 1 < nblk else None
            self.gemm(self.w2[b * KB * 128:(b + 1) * KB * 128, :], half, KB, [[(c, 512)] for c in range(0, D, 512)],
                      self.tiles, epi2, key="w2_%d" % b, then=then)

    def layernorm(self, which):
        E, nc = self.E, self.nc
        zv = self.z.rearrange("(kc p) t -> p kc t", p=128)
        ln_tiles = [(c, 256, 0) for c in range(0, TOWN, 256)] + ([(TOWN, self.NC, 1)] if self.NC else [])
        zti = lambda t0: 0 if t0 < 512 else (1 if t0 < 1024 else 2)
        with ExitStack() as st:
            lzr = Ring(E, "lnz", [st.enter_context(self.nc_sbuf("lnz%d_%d" % (which, i), [128, KC, 256], F32)) for i in range(2)])
            stat = st.enter_context(self.nc_sbuf("lnstat%d" % which, [128, 3, 256], F32))
            statb = Buf("lnstat")
            vec = st.enter_context(self.nc_sbuf("lnvec%d" % which, [128, 4, KC, 2], F32))
            vecb = Buf("lnvec")
            g_ap, b_ap = self.lng_sb[:, which, :], self.lnb_sb[:, which, :]
            if which == 0:
                for sel in range(2):
                    s2, sh2 = self.mod[:, 4 * KC:5 * KC, sel], self.mod[:, 3 * KC:4 * KC, sel]
                    E.op("dve", lambda e: e.tensor_tensor(out=vec[:, 0, :, sel], in0=g_ap, in1=s2, op=ALU.mult),
                         reads=[self.cst_b, self.mod_b], writes=[vecb])
                    E.op("dve", lambda e: e.tensor_tensor(out=vec[:, 1, :, sel], in0=b_ap, in1=s2, op=ALU.mult),
                         reads=[self.cst_b, self.mod_b], writes=[vecb])
                    E.op("dve", lambda e: e.tensor_tensor(out=vec[:, 1, :, sel], in0=vec[:, 1, :, sel], in1=sh2, op=ALU.add),
                         reads=[self.mod_b, vecb], writes=[vecb])
                E.op("dve", lambda e: e.tensor_scalar(out=vec[:, 2, :, 0], in0=g_ap, scalar1=float(DN_ALPHA), scalar2=None, op0=ALU.mult),
                     reads=[self.cst_b], writes=[vecb])
                E.op("dve", lambda e: e.tensor_scalar(out=vec[:, 3, :, 0], in0=b_ap, scalar1=float(DN_ALPHA), scalar2=None, op0=ALU.mult),
                     reads=[self.cst_b], writes=[vecb])
            def ld(idx):
                t0_, tn_, _ = ln_tiles[idx]
                lz_, lzb_, s_ = lzr.next()
                E.dma("sp", s_, lz_[:, :, 0:tn_], zv[:, :, t0_:t0_ + tn_], writes=[lzb_])
                return lz_, lzb_, s_
            nxt_ld = ld(0)
            for idx, (t0, tn, sel) in enumerate(ln_tiles):
                ti = zti(t0)
                lz, lzb1, s_lz = nxt_ld
                lzb = [lzb1]
                if idx + 1 < len(ln_tiles):
                    nxt_ld = ld(idx + 1)
                b1, b2 = E.bank(), E.bank()
                p1, p2 = self.ps[:, b1, 0:tn], self.ps[:, b2, 0:tn]

                def mm1(e):
                    for k in range(KC):
                        ins = e.matmul(p1, lhsT=self.ones_f[:], rhs=lz[:, k, 0:tn], start=(k == 0), stop=(k == KC - 1))
                    return ins
                E.op("pe", mm1, reads=lzb + [self.ones_buf], writes=[E.psb[b1]])
                for k in range(KC):
                    sq, sqb, _ = self.stg.next()
                    E.op("act", lambda e: e.activation(out=sq[:, 0:tn], in_=lz[:, k, 0:tn], func=AF.Square),
                         reads=lzb, writes=[sqb])
                    E.op("pe", lambda e: e.matmul(p2, lhsT=self.ones_f[:], rhs=sq[:, 0:tn], start=(k == 0), stop=(k == KC - 1)),
                         reads=[sqb, self.ones_buf], writes=[E.psb[b2]])
                mean, msq, rstd = stat[:, 0, 0:tn], stat[:, 1, 0:tn], stat[:, 2, 0:tn]
                E.op("dve", lambda e: e.tensor_scalar(out=mean, in0=p1, scalar1=1.0 / D, scalar2=None, op0=ALU.mult),
                     reads=[E.psb[b1]], writes=[statb])
                E.op("dve", lambda e: e.tensor_tensor(out=msq, in0=mean, in1=mean, op=ALU.mult), reads=[statb], writes=[statb])
                E.op("dve", lambda e: e.scalar_tensor_tensor(out=rstd, in0=p2, scalar=1.0 / D, in1=msq, op0=ALU.mult,
                                                             op1=ALU.subtract), reads=[E.psb[b2], statb], writes=[statb])
                E.op("dve", lambda e: e.tensor_scalar(out=rstd, in0=rstd, scalar1=float(LN_EPS), scalar2=None, op0=ALU.add),
                     reads=[statb], writes=[statb])
                E.op("act", lambda e: e.activation(out=rstd, in_=rstd, func=AF.Sqrt), reads=[statb], writes=[statb])
                E.op("dve", lambda e: e.reciprocal(out=rstd, in_=rstd), reads=[statb], writes=[statb])
                kb = [Buf("lzk%d" % k) for k in range(KC)]
                for k in range(KC):
                    zk = lz[:, k, 0:tn]
                    E.op("dve", lambda e: e.tensor_tensor(out=zk, in0=zk, in1=mean, op=ALU.subtract),
                         reads=[statb] + lzb, writes=[kb[k]])
                    E.op("dve", lambda e: e.tensor_tensor(out=zk, in0=zk, in1=rstd, op=ALU.mult),
                         reads=[statb, kb[k]], writes=[kb[k]])
                    if which == 0:
                        E.op("act", lambda e: e.activation(out=self.insb[:, k, t0:t0 + tn], in_=zk, func=AF.Identity,
                                                           scale=vec[:, 0, k, sel:sel + 1], bias=vec[:, 1, k, sel:sel + 1]),
                             reads=[kb[k], vecb], writes=[self.insb_b[k]])
                        E.op("act", lambda e: e.activation(out=zk, in_=zk, func=AF.Identity, scale=vec[:, 2, k, 0:1],
                                                           bias=vec[:, 3, k, 0:1]), reads=[kb[k], vecb], writes=[kb[k]])
                    else:
                        E.op("act", lambda e: e.activation(out=zk, in_=zk, func=AF.Identity, scale=self.lng_sb[:, which, k:k + 1],
                                                           bias=self.lnb_sb[:, which, k:k + 1]), reads=[kb[k], self.cst_b], writes=[kb[k]])
                if which == 0:
                    E.dma("sp", s_lz, zv[:, :, t0:t0 + tn], lz[:, :, 0:tn], reads=kb + lzb, writes=lzb)
                else:
                    if sel == 0:
                        dst = self.xo.rearrange("(kc p) t -> p kc t", p=128)[:, :, t0:t0 + tn]
                    else:
                        dst = self.co.rearrange("(kc p) t -> p kc t", p=128)[:, :, 0:tn]
                    E.dma("sp", s_lz, dst, lz[:, :, 0:tn], reads=kb + lzb, writes=lzb)
            E.barrier()

    def exchange(self):
        E, nc, fz, li = self.E, self.nc, self.fused, self.li
        E.barrier()
        G_b = fz["G_b"]
        if not getattr(self, "NOCOLL", False):
            E.coll(lambda e: e.collective_compute("AllGather", ALU.bypass, replica_groups=[list(range(NCORES))],
                                                  ins=[fz["bnd"]], outs=[fz["G"]]), reads=[fz["bnd_b"]], writes=[G_b])
        if getattr(self, "STOPCOLL", False):
            E.barrier()
            return
        Gv = fz["G"].rearrange("(r s kc p) t -> p r s kc t", r=NCORES, s=2, p=128)
        xn = fz["xext"][li + 1]
        with ExitStack() as st:
            selt = st.enter_context(self.nc_sbuf("sel_sb", [128, 2, NCORES], F32))
            selb = Buf("sel")
            E.dma("pool", E.new_sem("s_sel"), selt[:], fz["sel"], writes=[selb])
            gl = Ring(E, "gl", [st.enter_context(self.nc_sbuf("gl%d" % i, [128, NCORES, 128], F32)) for i in range(2)])
            ga = Ring(E, "ga", [st.enter_context(self.nc_sbuf("ga%d" % i, [128, 128], F32)) for i in range(2)])
            xb = Buf("xext_halo")
            for side in range(2):
                for kc in range(KC):
                    t, b, sm = gl.next()
                    E.dma("pool", sm, t[:], Gv[:, :, 1 - side, kc, :], reads=[G_b], writes=[b])
                    a, ab, asem = ga.next()
                    E.op("dve", lambda e: e.tensor_scalar(out=a[:], in0=t[:, 0, :], scalar1=selt[:, side, 0:1], scalar2=None,
                                                          op0=ALU.mult), reads=[b, selb], writes=[ab])
                    for r in range(1, NCORES):
                        E.op("dve", lambda e: e.scalar_tensor_tensor(out=a[:], in0=t[:, r, :], scalar=selt[:, side, r:r + 1], in1=a[:],
                                                                     op0=ALU.mult, op1=ALU.add), reads=[b, selb, ab], writes=[ab])
                    col0 = 0 if side == 0 else 128 + TOWN
                    E.dma("pool", asem, xn[kc * 128:(kc + 1) * 128, col0:col0 + 128], a[:], reads=[ab], writes=[xb])
            E.barrier()

    def attention(self):
        E, nc = self.E, self.nc
        NQ = self.T
        scale = float(HD ** -0.5)
        with ExitStack() as st:
            sbt = lambda n, s, dt: st.enter_context(self.nc_sbuf(n, s, dt))
            cos, sin = sbt("cos_sb", [128, 1280], F32), sbt("sin_sb", [128, 1280], F32)
            perm = sbt("perm_sb", [128, 128], F32)
            masks = sbt("masks_sb", [128, 4, 512], BF16)
            esink = sbt("esink", [128, NH], F32)
            qg = sbt("qg", [128, 4, 1280], BF16)
            kg = sbt("kg", [128, 1536], BF16)
            vg = sbt("vg", [128, 12, 128], BF16)
            pt = [sbt("pt%d" % i, [128, 5, 512], BF16) for i in range(2)]
            rden = [sbt("rden%d" % i, [128, 512], F32) for i in range(2)]
            acst = Buf("acst")
            s_acst = E.new_sem("s_acst")
            qgb = [Buf("qg%d" % h) for h in range(4)]
            kgb, vgb = Buf("kg"), Buf("vg")
            ptb = [Buf("pt%d" % i) for i in range(2)]
            rdb = [Buf("rden%d" % i) for i in range(2)]
            for dst, src in ((cos, self.cos), (sin, self.sin), (perm, self.perm)):
                E.dma("sp", s_acst, dst[:], src, writes=[acst])
            E.dma("pool", E.new_sem("s_acst_p"), masks[:], self.masks, writes=[acst])
            E.dma("sp", s_acst, esink[:], self.sinkb, writes=[acst])
            E.op("act", lambda e: e.activation(out=esink[:], in_=esink[:], func=AF.Exp), reads=[acst], writes=[acst])

            def rope(pap, pbuf, tn, dst_ap, dst_buf, e0):
                qf, qfb, _ = self.stg.next()
                E.op("act", lambda e: e.activation(out=qf[:, 0:tn], in_=pap, func=AF.Copy), reads=[pbuf], writes=[qfb])
                bk = E.bank()
                psw = self.ps[:, bk, 0:tn]
                E.op("pe", lambda e: e.matmul(psw, lhsT=perm[:], rhs=qf[:, 0:tn], start=True, stop=True),
                     reads=[qfb, acst], writes=[E.psb[bk]])
                t2, t2b, _ = self.stg.next()
                E.op("dve", lambda e: e.tensor_tensor(out=t2[:, 0:tn], in0=psw, in1=sin[:, e0:e0 + tn], op=ALU.mult),
                     reads=[E.psb[bk], acst], writes=[t2b])
                E.op("dve", lambda e: e.tensor_tensor(out=qf[:, 0:tn], in0=qf[:, 0:tn], in1=cos[:, e0:e0 + tn], op=ALU.mult),
                     reads=[qfb, acst], writes=[qfb])
                E.op("dve", lambda e: e.tensor_tensor(out=dst_ap, in0=qf[:, 0:tn], in1=t2[:, 0:tn], op=ALU.add),
                     reads=[qfb, t2b], writes=[dst_buf])

            q_tiles = [(128, 512, 0), (640, 512, 0)] + ([(1280, NCO, 1)] if self.NC else [])
            k_tiles = [(0, 512, 0), (512, 512, 0), (1024, 256, 0), (1280, 256, 1)]
            v_tiles = [(c * 128, 128, 0) for c in range(12)]
            n_qblk = 8 + (1 if self.NC else 0)
            it = 0
            for g in range(NKV):
                def epi_q(si, ci, ti, tile, pap, pbuf):
                    t0, tn, sel = tile
                    h = si * 2 + ci
                    if sel == 0:
                        rope(pap, pbuf, tn, qg[:, h, t0 - 128:t0 - 128 + tn], qgb[h], t0)
                    else:
                        E.op("act", lambda e: e.activation(out=qg[:, h, 1024:1024 + tn], in_=pap, func=AF.Copy),
                             reads=[pbuf], writes=[qgb[h]])
                self.gemm(self.wqkv, 0, KC, [[(512 * g, 256)], [(512 * g + 256, 256)]], q_tiles, epi_q, key="q%d" % g,
                          then=("kv%d" % g, self.wqkv, KC, [(4096 + 128 * g, 128), (5120 + 128 * g, 128)]))

                def epi_kv(si, ci, ti, tile, pap, pbuf):
                    t0, tn, sel = tile
                    if ci == 0:
                        if sel == 0:
                            rope(pap, pbuf, tn, kg[:, t0:t0 + tn], kgb, t0)
                        else:
                            E.op("act", lambda e: e.activation(out=kg[:, t0:t0 + tn], in_=pap, func=AF.Copy),
                                 reads=[pbuf], writes=[kgb])
                    else:
                        E.op("act", lambda e: e.activation(out=vg[:, t0 // 128, :], in_=pap, func=AF.Copy),
                             reads=[pbuf], writes=[vgb])
                then = ("q%d" % (g + 1), self.wqkv, KC, [(512 * (g + 1), 256)]) if g + 1 < NKV else ("oproj", self.wo, KC, [(0, self.SN)])
                self.gemm(self.wqkv, 0, KC, [[(4096 + 128 * g, 128), (5120 + 128 * g, 128)]], k_tiles, epi_kv,
                          mode=lambda si, ci: "N" if ci == 0 else v_tiles, key="kv%d" % g, then=then)

                for qb in range(n_qblk):
                    if qb < 8:
                        kch = [qb, qb + 1, qb + 2, 10, 11]
                        mk = [0 if qb == 0 else 1, None, 3 if qb == 7 else 2, None, None]
                    else:
                        kch = [10, 11]
                        mk = [None, None]
                    q0 = qb * 128
                    qw = 128 if qb < 8 else NCO
                    nw = 4 * qw
                    P, Pb = pt[it % 2], ptb[it % 2]
                    R, Rb = rden[it % 2], rdb[it % 2]
                    it += 1
                    for j, kc_ in enumerate(kch):
                        bk = E.bank()
                        sp_ = self.ps[:, bk, 0:nw]
                        E.op("pe", lambda e: e.matmul(sp_, lhsT=kg[:, kc_ * 128:(kc_ + 1) * 128], rhs=qg[:, :, q0:q0 + qw],
                                                      start=True, stop=True), reads=[kgb] + qgb, writes=[E.psb[bk]])
                        E.op("act", lambda e: e.activation(out=P[:, j, 0:nw], in_=sp_, func=AF.Exp, scale=scale),
                             reads=[E.psb[bk]], writes=[Pb])
                        if mk[j] is not None:
                            E.op("dve", lambda e: e.tensor_tensor(out=P[:, j, :], in0=P[:, j, :], in1=masks[:, mk[j], :],
                                                                  op=ALU.mult), reads=[Pb, acst], writes=[Pb])
                    bd, bo = E.bank(), E.bank()
                    pd, po = self.ps[:, bd, 0:nw], self.ps[:, bo, 0:nw]

                    def mmd(e):
                        for j in range(len(kch)):
                            ins = e.matmul(pd, lhsT=self.ones_b[:], rhs=P[:, j, 0:nw], start=(j == 0), stop=(j == len(kch) - 1))
                        return ins
                    E.op("pe", mmd, reads=[Pb, self.ones_buf], writes=[E.psb[bd]])

                    def mmo(e):
                        for j, kc_ in enumerate(kch):
                            ins = e.matmul(po, lhsT=vg[:, kc_, :], rhs=P[:, j, 0:nw], start=(j == 0), stop=(j == len(kch) - 1))
                        return ins
                    E.op("pe", mmo, reads=[Pb, vgb], writes=[E.psb[bo]])
                    for h in range(4):
                        E.op("dve", lambda e: e.tensor_scalar(out=R[:, h * qw:(h + 1) * qw], in0=pd[:, h * qw:(h + 1) * qw],
                                                              scalar1=esink[:, 4 * g + h:4 * g + h + 1], scalar2=None,
                                                              op0=ALU.add), reads=[E.psb[bd], acst], writes=[Rb])
                    E.op("dve", lambda e: e.reciprocal(out=R[:, 0:nw], in_=R[:, 0:nw]), reads=[Rb], writes=[Rb])
                    ob, obf, osem = self.stgb.next()
                    E.op("dve", lambda e: e.tensor_tensor(out=ob[:, 0:nw], in0=po, in1=R[:, 0:nw], op=ALU.mult),
                         reads=[E.psb[bo], Rb], writes=[obf])
                    dst = self.obuf[g * 512:(g + 1) * 512, q0:q0 + qw].rearrange("(h p) q -> p h q", p=128)
                    E.dma("sp", osem, dst, ob[:, 0:nw].rearrange("p (h q) -> p h q", h=4), reads=[obf],
                          writes=self.ob[4 * g:4 * g + 4])
            E.barrier()
            self.out_proj(self.wo)

    def out_proj(self, W):
        E = self.E
        ov = self.obuf.rearrange("(kc p) t -> p kc t", p=128)
        E.dma("sp", self.s_insb, self.insb[:, :, 0:self.T], ov, reads=self.ob, writes=self.insb_b)

        SN = self.SN

        def epi(si, ci, ti, tile, pap, pbuf):
            n = si * (SN // 128) + ci
            self.acc_z(n, ti, tile, pap, pbuf, self.mv(2, n, tile[2]))
        self.gemm(W, 0, KC, [[(c, SN)] for c in range(0, D, SN)], self.tiles, epi, key="oproj")

    def conv_mixer(self):
        E, nc = self.E, self.nc
        XE, CE = self.XE, self.CE
        L = XE + CE
        c_tiles = [(0, 512, 0), (512, 512, 0), (1024, XE - 1024, 0), (XE, CE, 1)]
        with ExitStack() as st:
            sbt = lambda n, s, dt: st.enter_context(self.nc_sbuf(n, s, dt))
            cw, cb = sbt("cw", [128, 3, KC], F32), sbt("cb", [128, KC], F32)
            Bs = [sbt("Bs%d" % i, [128, L], F32) for i in range(2)]
            Cs = [sbt("Cs%d" % i, [128, L], F32) for i in range(2)]
            Vs = [sbt("Vs%d" % i, [128, L], F32) for i in range(2)]
            Ys = [sbt("Ys%d" % i, [128, L], F32) for i in range(2)]
            Gs = [sbt("Gs%d" % i, [128, L], BF16) for i in range(2)]
            Bb = [Buf("Bs%d" % i) for i in range(2)]
            Cb = [Buf("Cs%d" % i) for i in range(2)]
            Vb = [Buf("Vs%d" % i) for i in range(2)]
            Yb = [Buf("Ys%d" % i) for i in range(2)]
            Gb = [Buf("Gs%d" % i) for i in range(2)]
            gsem = [E.new_sem("gs%d" % i) for i in range(2)]
            ccst = Buf("ccst")
            s_c = E.new_sem("s_ccst")
            E.dma("sp", s_c, cw[:], self.convw, writes=[ccst])
            E.dma("sp", s_c, cb[:], self.convb, writes=[ccst])

            def epi(si, ci, ti, tile, pap, pbuf):
                t0, tn, sel = tile
                p = si % 2
                if ci == 0:
                    E.op("act", lambda e: e.activation(out=Bs[p][:, t0:t0 + tn], in_=pap, func=AF.Copy), reads=[pbuf], writes=[Bb[p]])
                elif ci == 1:
                    E.op("act", lambda e: e.activation(out=Cs[p][:, t0:t0 + tn], in_=pap, func=AF.Copy), reads=[pbuf], writes=[Cb[p]])
                else:
                    E.op("dve", lambda e: e.tensor_tensor(out=Vs[p][:, t0:t0 + tn], in0=pap, in1=Cs[p][:, t0:t0 + tn], op=ALU.mult),
                         reads=[pbuf, Cb[p]], writes=[Vb[p]])
                    if ti == len(c_tiles) - 1:
                        n = si
                        V, Y, B, G = Vs[p], Ys[p], Bs[p], Gs[p]
                        E.op("dve", lambda e: e.tensor_scalar(out=Y[:, 1:L - 1], in0=V[:, 0:L - 2], scalar1=cw[:, 0, n:n + 1],
                                                              scalar2=None, op0=ALU.mult), reads=[Vb[p], ccst], writes=[Yb[p]])
                        E.op("dve", lambda e: e.scalar_tensor_tensor(out=Y[:, 1:L - 1], in0=V[:, 1:L - 1], scalar=cw[:, 1, n:n + 1],
                                                                     in1=Y[:, 1:L - 1], op0=ALU.mult, op1=ALU.add),
                             reads=[Vb[p], ccst, Yb[p]], writes=[Yb[p]])
                        E.op("dve", lambda e: e.scalar_tensor_tensor(out=Y[:, 1:L - 1], in0=V[:, 2:L], scalar=cw[:, 2, n:n + 1],
                                                                     in1=Y[:, 1:L - 1], op0=ALU.mult, op1=ALU.add),
                             reads=[Vb[p], ccst, Yb[p]], writes=[Yb[p]])
                        E.op("dve", lambda e: e.scalar_tensor_tensor(out=G[:, 1:L - 1], in0=Y[:, 1:L - 1], scalar=cb[:, n:n + 1],
                                                                     in1=B[:, 1:L - 1], op0=ALU.add, op1=ALU.mult),
                             reads=[Yb[p], ccst, Bb[p]], writes=[Gb[p]])
                        E.dma("sp", gsem[p], self.obuf[n * 128:(n + 1) * 128, 0:TOWN], G[:, 1:1 + TOWN], reads=[Gb[p]], writes=[self.ob[n]])
                        if self.NC:
                            E.dma("sp", gsem[p], self.obuf[n * 128:(n + 1) * 128, TOWN:TOWN + NCO], G[:, XE + 1:XE + 1 + NCO],
                                  reads=[Gb[p]], writes=[self.ob[n]])
            slabs = [[(n * 128, 128), (D + n * 128, 128), (2 * D + n * 128, 128)] for n in range(KC)]
            self.gemm(self.win, 0, KC, slabs, c_tiles, epi, then=("oproj", self.wout, KC, [(0, self.SN)]))
            E.barrier()
            self.out_proj(self.wout)

    def prep_pool(self):
        E, nc = self.E, self.nc
        XE, CE, HL, CP = self.XE, self.CE, self.HL, self.CP
        L = XE + CE
        xv = self.xT.rearrange("(kc p) t -> p kc t", p=128)
        cv = self.cT.rearrange("(kc p) t -> p kc t", p=128)
        zv = self.z.rearrange("(kc p) t -> p kc t", p=128)
        with ExitStack() as st:
            sbt = lambda n, s, dt: st.enter_context(self.nc_sbuf(n, s, dt))
            xl = Ring(E, "xl", [sbt("xl%d" % i, [128, L], F32) for i in range(2)])
            zs = Ring(E, "zs", [sbt("zs%d" % i, [128, TOWN + NCTX], F32) for i in range(2)])
            invc = sbt("invc_sb", [128, 4, L], F32)
            U = [sbt("pU%d" % i, [128, L], F32) for i in range(2)]
            A = [sbt("pA%d" % i, [128, L], F32) for i in range(2)]
            Bq = [sbt("pB%d" % i, [128, L], F32) for i in range(2)]
            Ub = [Buf("pU%d" % i) for i in range(2)]
            Ab = [Buf("pA%d" % i) for i in range(2)]
            Bb = [Buf("pB%d" % i) for i in range(2)]
            icb = Buf("invc")
            s_ic = E.new_sem("s_invc")
            E.dma("sp", s_ic, invc[:], self.invc, writes=[icb])
            def ldx(kc_):
                t_, b_, s_ = xl.next()
                E.dma("sp", s_, t_[:, 0:XE], xv[:, kc_, :], writes=[b_])
                E.dma("sp", s_, t_[:, XE:L], cv[:, kc_, :], writes=[b_])
                return t_, b_, s_
            nx = ldx(0)
            for kc in range(KC):
                p = kc % 2
                gi = kc // 8
                t, b, s = nx
                if kc + 1 < KC:
                    nx = ldx(kc + 1)
                u, ub = U[p], Ub[p]
                E.op("act", lambda e: e.activation(out=u[:, 0:XE], in_=t[:, 0:XE], func=AF.Identity,
                                                   scale=self.mv(1, kc, 0), bias=self.mv(0, kc, 0)), reads=[b, self.mod_b], writes=[ub])
                E.op("act", lambda e: e.activation(out=u[:, XE:L], in_=t[:, XE:L], func=AF.Identity,
                                                   scale=self.mv(1, kc, 1), bias=self.mv(0, kc, 1)), reads=[b, self.mod_b], writes=[ub])
                E.op("dve", lambda e: e.tensor_scalar(out=u[:, 0:HL], in0=u[:, 0:HL], scalar1=self.edge_sb[:, 0:1], scalar2=None,
                                                      op0=ALU.mult), reads=[self.cst_b, ub], writes=[ub])
                E.op("dve", lambda e: e.tensor_scalar(out=u[:, HL + TOWN:XE], in0=u[:, HL + TOWN:XE], scalar1=self.edge_sb[:, 1:2],
                                                      scalar2=None, op0=ALU.mult), reads=[self.cst_b, ub], writes=[ub])
                E.op("dve", lambda e: e.tensor_scalar(out=u[:, XE:XE + CP], in0=u[:, XE:XE + CP], scalar1=self.edge_sb[:, 0:1],
                                                      scalar2=None, op0=ALU.mult), reads=[self.cst_b, ub], writes=[ub])
                E.op("dve", lambda e: e.tensor_scalar(out=u[:, XE + CP + NCO:L], in0=u[:, XE + CP + NCO:L], scalar1=self.edge_sb[:, 1:2],
                                                      scalar2=None, op0=ALU.mult), reads=[self.cst_b, ub], writes=[ub])
                a, ab, bq, bb = A[p], Ab[p], Bq[p], Bb[p]
                E.op("dve", lambda e: e.tensor_tensor(out=a[:, 1:L], in0=u[:, 0:L - 1], in1=u[:, 1:L], op=ALU.add),
                     reads=[ub], writes=[ab])
                cur, curb, oth, othb = a, ab, bq, bb
                lo, hi = 1, L
                for step, (dl, dr) in enumerate(((1, 1), (2, 2), (4, 4))):
                    if step >= gi:
                        break
                    nlo, nhi = lo + dl, hi - dr
                    E.op("dve", lambda e: e.tensor_tensor(out=oth[:, nlo:nhi], in0=cur[:, nlo - dl:nhi - dl],
                                                          in1=cur[:, nlo + dr:nhi + dr], op=ALU.add), reads=[curb], writes=[othb])
                    cur, curb, oth, othb = oth, othb, cur, curb
                    lo, hi = nlo, nhi
                E.op("dve", lambda e: e.tensor_tensor(out=cur[:, 8:L - 8], in0=cur[:, 8:L - 8], in1=invc[:, gi, 8:L - 8], op=ALU.mult),
                     reads=[curb, icb], writes=[curb])
                ib = self.insb_b[kc]
                E.op("dve", lambda e: e.tensor_tensor(out=self.insb[:, kc, 0:TOWN], in0=cur[:, HL:HL + TOWN], in1=u[:, HL:HL + TOWN],
                                                      op=ALU.subtract), reads=[curb, ub], writes=[ib])
                E.op("dve", lambda e: e.tensor_tensor(out=self.insb[:, kc, TOWN:TOWN + NCO], in0=cur[:, XE + CP:XE + CP + NCO],
                                                      in1=u[:, XE + CP:XE + CP + NCO], op=ALU.subtract), reads=[curb, ub], writes=[ib])
                zt, zb_, zsem = zs.next()
                E.op("act", lambda e: e.activation(out=zt[:, 0:TOWN], in_=t[:, HL:HL + TOWN], func=AF.Copy, scale=float(DN_ALPHA)),
                     reads=[b], writes=[zb_])
                E.op("act", lambda e: e.activation(out=zt[:, TOWN:TOWN + NCO], in_=t[:, XE + CP:XE + CP + NCO], func=AF.Copy,
                                                   scale=float(DN_ALPHA)), reads=[b], writes=[zb_])
                E.dma("sp", zsem, zv[:, kc, :], zt[:, 0:self.T], reads=[zb_], writes=self.zb[kc])
            E.barrier()

    def pool_mixer(self):
        E, nc = self.E, self.nc
        with ExitStack() as st:
            psc = st.enter_context(self.nc_sbuf("psc", [128, KC], F32))
            pg = st.enter_context(self.nc_sbuf("pg", [128, KC, 2], F32))
            pgb = Buf("pg")
            s_p = E.new_sem("s_psc")
            E.dma("sp", s_p, psc[:], self.pscale, writes=[pgb])
            for sel in range(2):
                E.op("dve", lambda e: e.tensor_tensor(out=pg[:, :, sel], in0=self.mod[:, 2 * KC:3 * KC, sel], in1=psc[:], op=ALU.mult),
                     reads=[pgb, self.mod_b], writes=[pgb])
            for gi in range(4):
                def epi(si, ci, ti, tile, pap, pbuf):
                    n = gi * 8 + si * 4 + ci
                    self.acc_z(n, ti, tile, pap, pbuf, pg[:, n, tile[2]:tile[2] + 1], extra_reads=[pgb])
                self.gemm(self.poolw[gi], gi * 8, 8, [[(c, 512)] for c in range(0, 1024, 512)], self.tiles, epi, key="p%d" % gi,
                          then=(("p%d" % (gi + 1), self.poolw[gi + 1], 8, [(0, 512)]) if gi < 3 else None))
            E.barrier()


def _pm(v):
    v = np.asarray(v, np.float32)
    lead = v.shape[:-1]
    r = v.reshape(lead + (KC, 128))
    return np.ascontiguousarray(np.moveaxis(r, -1, 0))


def _rope_tables(core):
    e = np.arange(1280)
    t = core * TOWN - 128 + e
    row = (t // 64).astype(np.float32)
    col = (t % 64).astype(np.float32)
    freqs = np.power(np.float32(10000.0), -np.arange(32, dtype=np.float32) / np.float32(32)).astype(np.float32)
    ang = np.stack([row[:, None] * freqs[None, :], col[:, None] * freqs[None, :]], axis=0).astype(np.float32)
    cosv, sinv = np.cos(ang).astype(np.float32), np.sin(ang).astype(np.float32)
    cos = np.zeros((128, 1280), np.float32)
    sin = np.zeros((128, 1280), np.float32)
    for a in range(2):
        for b in range(2):
            cos[a * 64 + b * 32:a * 64 + b * 32 + 32, :] = cosv[a].T
            sin[a * 64 + b * 32:a * 64 + b * 32 + 32, :] = sinv[a].T
    return cos, sin


def _perm():
    p = np.zeros((128, 128), np.float32)
    for a in range(2):
        for f in range(32):
            p[a * 64 + 32 + f, a * 64 + f] = -1.0
            p[a * 64 + f, a * 64 + 32 + f] = 1.0
    return p


def _masks(core):
    j = np.arange(128)[:, None]
    q = np.arange(128)[None, :]
    mp = (j >= q).astype(np.float32)
    mn = (j <= q).astype(np.float32)
    m = np.stack([mp * (1.0 if core > 0 else 0.0), mp, mn, mn * (1.0 if core < NCORES - 1 else 0.0)], axis=1)
    return np.ascontiguousarray(np.broadcast_to(m[:, :, None, :], (128, 4, 4, 128)).reshape(128, 4, 512))


def _invc(core, XE, CE):
    L = XE + CE
    out = np.ones((4, L), np.float32)
    for gi, w in enumerate(POOL_WINDOWS):
        for (off, n, S, t0) in ((0, XE, SEQ, core * TOWN - 8), (XE, CE, NCTX, core * NCO - 8)):
            t = t0 + np.arange(n)
            lo = np.clip(t - w // 2, 0, S)
            hi = np.clip(t + w - w // 2, 0, S)
            cnt = (hi - lo).astype(np.float32)
            ok = (t >= 0) & (t < S)
            out[gi, off:off + n] = np.where(ok, np.float32(1.0) / np.maximum(cnt, 1.0), np.float32(1.0))
    return np.ascontiguousarray(np.broadcast_to(out[None], (128, 4, L)))


_PROGS = {}


def _get_prog(kind, last, mod_in, make_next):
    key = (kind, last, mod_in, make_next)
    if key not in _PROGS:
        nc = bass.Bass("TRN2", target_bir_lowering=False)
        LayerProg(nc, kind, last, mod_in=mod_in, make_next=make_next).build()
        _PROGS[key] = nc
    return _PROGS[key]


def _layer_inputs(i, x, ctx, inp, sfx="_L0", with_x=True, modin=None, make_next=False):
    kind, j = KINDS[i], i // 3
    last = i == DEPTH - 1
    HL, CP = HALO[kind], CPAD[kind]
    XE, CE = TOWN + 2 * HL, (NCTX if kind == 0 else NCO + 2 * CP)
    f = lambda a: np.ascontiguousarray(np.asarray(a, np.float32))
    xp = np.pad(x, ((HL, HL), (0, 0)))
    cpad = np.pad(ctx, ((CP, CP), (0, 0)))
    cc = np.ascontiguousarray(np.stack([_pm(inp["c"][0]), _pm(inp["c_ctx"])], axis=-1))
    common = {
        "cc": cc,
        "lng": _pm(inp["ln_g"][i]), "lnb": _pm(inp["ln_b"][i]),
        "w1": f(inp["mlp_w1"][i]), "w2": f(inp["mlp_w2"][i]),
    }
    if kind == 0:
        common.update({"wqkv": f(inp["attn_w_qkv"][j]), "wo": f(inp["attn_w_o"][j]),
                       "sinkb": np.ascontiguousarray(np.broadcast_to(np.asarray(inp["attn_sink"][j], np.float32)[None, :], (128, NH))),
                       "perm": _perm()})
    elif kind == 1:
        common.update({"poolw": f(inp["pool_w"][j]), "pscale": _pm(inp["pool_scale"][j])})
    else:
        common.update({"win": f(inp["conv_w_in"][j]), "convw": _pm(inp["conv_w"][j]), "convb": _pm(inp["conv_b"][j]),
                       "wout": f(inp["conv_w_out"][j])})
    if modin is not None:
        common["modin"] = modin
    else:
        common.update({"w_down": f(inp["mod_w_down"][i]), "w_up": f(inp["mod_w_up"][i]),
                       "modb": _pm(inp["mod_b"][i].reshape(6, D)).reshape(128, 6 * KC)})
    if make_next:
        common["w_down_n"] = f(inp["mod_w_down"][i + 1])
        modb_next = _pm(inp["mod_b"][i + 1].reshape(6, D)).reshape(128, 6 * KC)
        nn = 6 * KC // NCORES
    maps = []
    for c in range(NCORES):
        m = dict(common)
        if make_next:
            m["w_up_n"] = np.ascontiguousarray(np.asarray(inp["mod_w_up"][i + 1], np.float32)[:, c * nn * 128:(c + 1) * nn * 128])
            m["modb_n"] = np.ascontiguousarray(modb_next[:, c * nn:(c + 1) * nn])
        if with_x:
            m["xT"] = np.ascontiguousarray(xp[c * TOWN:c * TOWN + XE].T)
            if kind == 0:
                m["cT"] = np.ascontiguousarray(np.roll(ctx, -NCO * c, axis=0).T)
            else:
                m["cT"] = np.ascontiguousarray(cpad[c * NCO:c * NCO + CE].T)
        m["edge"] = np.ascontiguousarray(np.broadcast_to(
            np.array([[1.0 if c > 0 else 0.0, 1.0 if c < NCORES - 1 else 0.0]], np.float32), (128, 2)))
        if kind == 0:
            m["cos"], m["sin"] = _rope_tables(c)
            m["masks"] = _masks(c)
        elif kind == 1:
            m["invc"] = _invc(c, XE, CE)
        maps.append({k + sfx: v for k, v in m.items()})
    return maps


def run_layer(i, x, ctx, inp, modin=None):
    kind = KINDS[i]
    last = i == DEPTH - 1
    make_next = not last
    nc = _get_prog(kind, last, modin is not None, make_next)
    maps = _layer_inputs(i, x, ctx, inp, modin=modin, make_next=make_next)
    res = run_bass_kernel_spmd(nc, maps, core_ids=list(range(NCORES)))
    xn = np.concatenate([np.asarray(r["xo_L0"]).T for r in res.results], axis=0)
    cn = None if last else np.concatenate([np.asarray(r["co_L0"]).T for r in res.results], axis=0)
    mn = None if last else np.ascontiguousarray(np.concatenate([np.asarray(r["modn_L0"]) for r in res.results], axis=1))
    return xn, cn, mn


def build_mega():
    nc = bass.Bass("TRN2", target_bir_lowering=False)
    with ExitStack() as st0:
        E = Env(nc, st0)
        itn = lambda n, sh, dt=F32: nc.dram_tensor(n, sh, dt, kind="Internal").ap()
        fz = {"xext": {li: itn("xext%d" % li, [D, TOWN + 256]) for li in range(1, DEPTH)},
              "cext": {li: itn("cext%d" % li, [D, NCTX + 16]) for li in range(1, DEPTH)},
              "bnd": itn("bnd", [2 * D, 128]), "G": itn("Gall", [NCORES * 2 * D, 128]),
              "sel": nc.dram_tensor("sel", [128, 2, NCORES], F32, kind="ExternalInput").ap(),
              "bnd_b": Buf("bnd"), "G_b": Buf("G")}
        for li in range(DEPTH):
            lp = LayerProg(nc, KINDS[li], li == DEPTH - 1, li, E, fz)
            lp.build()
            if li < DEPTH - 1:
                lp.exchange()
        E.barrier()
    return nc


_MEGA = []


def kernel_fused(inp):
    if not _MEGA:
        _MEGA.append(build_mega())
    nc = _MEGA[0]
    x = np.ascontiguousarray(inp["x"][0], dtype=np.float32)
    ctx = np.ascontiguousarray(inp["ctx"][0], dtype=np.float32)
    maps = [dict() for _ in range(NCORES)]
    for i in range(DEPTH):
        lm = _layer_inputs(i, x, ctx, inp, sfx="_L%d" % i, with_x=(i == 0))
        for c in range(NCORES):
            maps[c].update(lm[c])
    for c in range(NCORES):
        selv = np.zeros((2, NCORES), np.float32)
        if c > 0:
            selv[0, c - 1] = 1.0
        if c < NCORES - 1:
            selv[1, c + 1] = 1.0
        maps[c]["sel"] = np.ascontiguousarray(np.broadcast_to(selv[None], (128, 2, NCORES)))
    res = run_bass_kernel_spmd(nc, maps, core_ids=list(range(NCORES)))
    xn = np.concatenate([np.asarray(r["xo_L%d" % (DEPTH - 1)]).T for r in res.results], axis=0)
    return xn[None].astype(np.float32)


def kernel_unfused(inp):
    x = np.ascontiguousarray(inp["x"][0], dtype=np.float32)
    ctx = np.ascontiguousarray(inp["ctx"][0], dtype=np.float32)
    modin = None
    for i in range(DEPTH):
        x, ctx, modin = run_layer(i, x, ctx, inp, modin=modin)
    return x[None].astype(np.float32)


FUSED = False


def kernel(**inputs):
    inp = {k: np.asarray(v) for k, v in inputs.items()}
    return kernel_fused(inp) if FUSED else kernel_unfused(inp)
```
